# Optimizing a Trainium2 kernel written in Bass

```python
import jax, jax.numpy as jnp
from jax import lax
import numpy as np

D_MODEL = 4096
BATCH = 4
SEQ = 2048
DEPTH = 2
DEC_BATCH = 32
DEC_SEQ = 4
PAST_LEN = 16384
PAGE_SIZE = 128

HEAD_DIM = 64
D_ATTN = D_MODEL // 2
N_Q_HEADS = D_ATTN // HEAD_DIM
N_KV_HEADS = N_Q_HEADS // 8
GQA_GROUP = N_Q_HEADS // N_KV_HEADS
D_KV = N_KV_HEADS * HEAD_DIM
WINDOW = 128
D_SGU = D_MODEL // 2
N_SGU_GROUPS = 8
SGU_GROUP_DIM = D_SGU // N_SGU_GROUPS
CHUNK = 128
ALPHA = (2 * DEPTH) ** 0.25
BETA = (8 * DEPTH) ** -0.25
LN_EPS = 1e-5
COL_SIZES = (D_ATTN, D_KV, D_KV, D_ATTN, D_SGU, D_SGU, D_SGU, D_MODEL, D_MODEL)
IN_COLS = D_ATTN + 2 * D_KV + D_ATTN + 3 * D_SGU + 2 * D_MODEL

kernel_name = "hybrid_swa_sink_chunk_sgu_decoder_step"


def layer_norm(x, gain, bias):
    xf = x.astype(jnp.float32)
    mu = jnp.mean(xf, axis=-1, keepdims=True)
    var = jnp.mean(jnp.square(xf - mu), axis=-1, keepdims=True)
    return ((xf - mu) * lax.rsqrt(var + LN_EPS)).astype(x.dtype) * gain + bias


def split_cols(proj):
    idx = np.cumsum(np.array(COL_SIZES))[:-1].tolist()
    return jnp.split(proj, idx, axis=-1)


def window_attention(q, k, v, sinks, q_pos, k_pos):
    scores = jnp.einsum('bnqkgd,bnlkd->bnkgql', q, k).astype(jnp.float32) * (HEAD_DIM ** -0.5)
    rel = q_pos[:, :, None] - k_pos[:, None, :]
    valid = (rel >= 0) & (rel < WINDOW) & (k_pos[:, None, :] >= 0)
    scores = jnp.where(valid[None, :, None, None], scores, -jnp.inf)
    sink = jnp.broadcast_to(sinks.astype(jnp.float32).reshape(1, 1, N_KV_HEADS, GQA_GROUP, 1, 1),
                            scores.shape[:-1] + (1,))
    probs = jax.nn.softmax(jnp.concatenate([scores, sink], axis=-1), axis=-1)[..., :-1]
    return jnp.einsum('bnkgql,bnlkd->bnqkgd', probs.astype(v.dtype), v)


def trunk_layer(x, c, w_ada, b_ada, w_in, sinks, sgu_g, sgu_b, w_s, b_s, w_pa, w_pb, w_o, ln_g, ln_b,
                cache_k=None, cache_v=None):
    B, T, _ = x.shape
    mod = jax.nn.silu(c) @ w_ada + b_ada
    shift, scale, gate = jnp.split(mod, 3, axis=-1)
    h = x * (1 + scale[:, None]) + shift[:, None]
    q, k, v, z_a, u, v_b, z_b, g_a, g_b = split_cols(h @ w_in)
    q = q.reshape(B, T, N_KV_HEADS, GQA_GROUP, HEAD_DIM)
    k = k.reshape(B, T, N_KV_HEADS, HEAD_DIM)
    v = v.reshape(B, T, N_KV_HEADS, HEAD_DIM)
    v_b = layer_norm(v_b, sgu_g, sgu_b)
    w_masked = w_s * jnp.tril(jnp.ones((CHUNK, CHUNK), w_s.dtype))
    if cache_k is None:
        nb = T // WINDOW
        pad = ((0, 0), (WINDOW, 0), (0, 0), (0, 0))
        kp = jnp.pad(k, pad).reshape(B, nb + 1, WINDOW, N_KV_HEADS, HEAD_DIM)
        vp = jnp.pad(v, pad).reshape(B, nb + 1, WINDOW, N_KV_HEADS, HEAD_DIM)
        k_blk = jnp.concatenate([kp[:, :-1], kp[:, 1:]], axis=2)
        v_blk = jnp.concatenate([vp[:, :-1], vp[:, 1:]], axis=2)
        q_blk = q.reshape(B, nb, WINDOW, N_KV_HEADS, GQA_GROUP, HEAD_DIM)
        starts = jnp.arange(nb) * WINDOW
        q_pos = starts[:, None] + jnp.arange(WINDOW)[None]
        k_pos = starts[:, None] - WINDOW + jnp.arange(2 * WINDOW)[None]
        attn = window_attention(q_blk, k_blk, v_blk, sinks, q_pos, k_pos)
        vc = v_b.reshape(B, T // CHUNK, CHUNK, N_SGU_GROUPS, SGU_GROUP_DIM)
        s = jnp.einsum('gts,bnsgc->bntgc', w_masked, vc) + b_s.T[None, None, :, :, None]
    else:
        k_blk = jnp.concatenate([cache_k, k], axis=1)[:, None]
        v_blk = jnp.concatenate([cache_v, v], axis=1)[:, None]
        q_pos = (PAST_LEN + jnp.arange(T))[None]
        k_pos = (PAST_LEN - WINDOW + jnp.arange(WINDOW + T))[None]
        attn = window_attention(q[:, None], k_blk, v_blk, sinks, q_pos, k_pos)
        vc = v_b.reshape(B, T, N_SGU_GROUPS, SGU_GROUP_DIM)
        s = jnp.einsum('gts,bsgc->btgc', w_masked[:, :T, :T], vc) + b_s[:, :T].T[None, :, :, None]
    attn = attn.reshape(B, T, D_ATTN) * jax.nn.silu(z_a)
    out_b = u * s.reshape(B, T, D_SGU) * jax.nn.silu(z_b)
    merged = jax.nn.sigmoid(g_a) * (attn @ w_pa) + jax.nn.sigmoid(g_b) * (out_b @ w_pb)
    y = merged @ w_o
    x = layer_norm(ALPHA * x + gate[:, None] * y, ln_g, ln_b)
    return x, k, v, v_b


def setup_inputs(seed: int = 0) -> dict:
    key = jax.random.key(seed)
    ks = jax.random.split(key, 20)
    f32 = jnp.float32
    nrm = lambda k, shape, s: jax.random.normal(k, shape, f32) * s
    return {
        "x_prompt": nrm(ks[0], (BATCH, SEQ, D_MODEL), 1.0),
        "x_sample": nrm(ks[1], (DEC_BATCH, DEC_SEQ, D_MODEL), 1.0),
        "cache_k": nrm(ks[2], (DEPTH, DEC_BATCH, WINDOW, N_KV_HEADS, HEAD_DIM), 1.0),
        "cache_v": nrm(ks[3], (DEPTH, DEC_BATCH, WINDOW, N_KV_HEADS, HEAD_DIM), 1.0),
        "c_prompt": nrm(ks[4], (BATCH, D_MODEL), 1.0),
        "c_sample": nrm(ks[5], (DEC_BATCH, D_MODEL), 1.0),
        "w_ada": nrm(ks[6], (DEPTH, D_MODEL, 3 * D_MODEL), 0.5 * D_MODEL ** -0.5),
        "b_ada": nrm(ks[7], (DEPTH, 3 * D_MODEL), 0.02),
        "w_in": nrm(ks[8], (DEPTH, D_MODEL, IN_COLS), D_MODEL ** -0.5),
        "attn_sinks": nrm(ks[9], (DEPTH, N_Q_HEADS), 1.0),
        "sgu_ln_gain": 1.0 + nrm(ks[10], (DEPTH, D_SGU), 0.05),
        "sgu_ln_bias": nrm(ks[11], (DEPTH, D_SGU), 0.02),
        "sgu_w_s": nrm(ks[12], (DEPTH, N_SGU_GROUPS, CHUNK, CHUNK), 0.5 * CHUNK ** -0.5),
        "sgu_b_s": 1.0 + nrm(ks[13], (DEPTH, N_SGU_GROUPS, CHUNK), 0.1),
        "w_pa": nrm(ks[14], (DEPTH, D_ATTN, D_MODEL), BETA * D_ATTN ** -0.5),
        "w_pb": nrm(ks[15], (DEPTH, D_SGU, D_MODEL), BETA * D_SGU ** -0.5),
        "w_o": nrm(ks[16], (DEPTH, D_MODEL, D_MODEL), BETA * D_MODEL ** -0.5),
        "ln_gain": 1.0 + nrm(ks[17], (DEPTH, D_MODEL), 0.05),
        "ln_bias": nrm(ks[18], (DEPTH, D_MODEL), 0.02),
    }


def reference(x_prompt, x_sample, cache_k, cache_v, c_prompt, c_sample, w_ada, b_ada, w_in, attn_sinks,
              sgu_ln_gain, sgu_ln_bias, sgu_w_s, sgu_b_s, w_pa, w_pb, w_o, ln_gain, ln_bias):
    xp, xs = x_prompt, x_sample
    win_k_p, win_v_p, new_k_s, new_v_s, sgu_v_s = [], [], [], [], []
    for l in range(DEPTH):
        params = (w_ada[l], b_ada[l], w_in[l], attn_sinks[l], sgu_ln_gain[l], sgu_ln_bias[l],
                  sgu_w_s[l], sgu_b_s[l], w_pa[l], w_pb[l], w_o[l], ln_gain[l], ln_bias[l])
        xp, kp, vp, _ = trunk_layer(xp, c_prompt, *params)
        xs, ks_, vs_, vbs = trunk_layer(xs, c_sample, *params, cache_k=cache_k[l], cache_v=cache_v[l])
        win_k_p.append(kp[:, -WINDOW:])
        win_v_p.append(vp[:, -WINDOW:])
        new_k_s.append(ks_)
        new_v_s.append(vs_)
        sgu_v_s.append(vbs)
    win_k_prompt = jnp.stack(win_k_p)
    win_v_prompt = jnp.stack(win_v_p)
    new_k_sample = jnp.stack(new_k_s)
    new_v_sample = jnp.stack(new_v_s)
    sgu_v_sample = jnp.stack(sgu_v_s)
    return (xp, xs, win_k_prompt, win_v_prompt, new_k_sample, new_v_sample, sgu_v_sample)
```

```python
import contextlib
import numpy as np
import concourse.bass as bass
import concourse.mybir as mybir
from concourse.bass_utils import run_bass_kernel_spmd

F32 = mybir.dt.float32
BF16 = mybir.dt.bfloat16
AF = mybir.ActivationFunctionType
ALU = mybir.AluOpType

D = 4096
NJ = 32
DEPTH = 2
NCORES = 8
NPB = 10
NS = 16
NT = NPB * 128 + NS
NM = NT - 128
ALPHA = (2 * DEPTH) ** 0.25
LN_EPS = 1e-5
NEG = -30000.0
DEBUG = False


class Op:
    __slots__ = ("eng", "fn", "deps", "is_dma", "group", "needs_inc", "tok", "idx")


class DmaGroup:
    __slots__ = ("chan", "ops", "val", "prev")

    def __init__(self, chan):
        self.chan = chan
        self.ops = []
        self.val = None
        self.prev = None


class Prog:
    ENGS = ("pe", "act", "dve", "pool", "sp")

    def __init__(self, sem_rot=4):
        self.ops = []
        self.state = {}
        self.chan_groups = {}
        self.sem_rot = sem_rot
        self.pending = []
        self.auto_n = 0

    def add(self, eng, fn, reads=(), writes=(), group=None, extra_deps=()):
        op = Op()
        op.eng = eng
        op.fn = fn
        op.is_dma = group is not None
        op.group = group
        op.needs_inc = op.is_dma
        op.tok = None
        op.idx = len(self.ops)
        deps = set()
        st = self.state
        psr = [k for k in reads if isinstance(k, tuple) and k and k[0] == "ps"]
        if psr:
            reads = [k for k in reads if k not in psr]
            writes = list(writes) + [k for k in psr if k not in writes]
        for k in reads:
            s = st.get(k)
            if s is not None and s[0] is not None:
                deps.add(s[0])
        for k in writes:
            s = st.get(k)
            if s is not None:
                if s[0] is not None:
                    deps.add(s[0])
                for r in s[1]:
                    deps.add(r)
        deps.update(extra_deps)
        fdeps = []
        for d in deps:
            if d is op:
                continue
            if (not d.is_dma) and d.eng == "pe" and eng == "pe" and not op.is_dma:
                continue
            fdeps.append(d)
            d.needs_inc = True
        op.deps = fdeps
        if group is not None:
            for d in fdeps:
                assert d.group is not group, "intra-group DMA dependency"
        for k in reads:
            s = st.get(k)
            if s is None:
                st[k] = [None, [op]]
            else:
                if not op.is_dma:
                    s[1] = [r for r in s[1] if r.is_dma or r.eng != eng]
                s[1].append(op)
        for k in writes:
            st[k] = [op, []]
        if group is not None:
            group.ops.append(op)
            if eng == "sp":
                self.pending.append(op)
        self.ops.append(op)
        return op

    NPOOL = 24

    def dma_group(self, chan):
        if chan is None or not str(chan).startswith("ring"):
            chan = "auto%d" % (self.auto_n % self.NPOOL)
            self.auto_n += 1
        g = DmaGroup(chan)
        lst = self.chan_groups.setdefault(chan, [])
        g.prev = lst[-1] if lst else None
        lst.append(g)
        return g

    def dma(self, queue, out, in_, reads=(), writes=(), chan=None, group=None, **kw):
        if group is None:
            group = self.dma_group(chan)
        ed = tuple(group.prev.ops) if group.prev is not None else ()
        return self.add(queue, lambda e: e.dma_start(out=out, in_=in_, **kw), reads, writes, group=group, extra_deps=ed)

    def emit(self, nc):
        R = self.sem_rot
        with contextlib.ExitStack() as es:
            esem = {e: [es.enter_context(nc.semaphore(f"p_{e}{i}")) for i in range(R)] for e in self.ENGS}
            csem = {}
            for chan, groups in self.chan_groups.items():
                csem[chan] = es.enter_context(nc.semaphore(f"d_{chan}"))
                cum = 0
                for g in groups:
                    cum += 16 * len(g.ops)
                    g.val = cum
                    for o in g.ops:
                        o.tok = (("c", chan), cum)
            cnt = {e: 0 for e in self.ENGS}
            for o in self.ops:
                if o.is_dma:
                    continue
                if o.needs_inc:
                    m = cnt[o.eng]
                    cnt[o.eng] += 1
                    o.tok = (("e", o.eng, m % R), m // R + 1)
            per_eng = {e: [o for o in self.ops if o.eng == e] for e in self.ENGS}
            block = es.enter_context(nc.Block())

            def run(e_name, e):
                known = {}
                for o in per_eng[e_name]:
                    need_e = {}
                    need_c = {}
                    for d in o.deps:
                        k, v = d.tok
                        if k[0] == "e":
                            m = (v - 1) * R + k[2]
                            if need_e.get(k[1], -1) < m:
                                need_e[k[1]] = m
                        else:
                            if need_c.get(k, 0) < v:
                                need_c[k] = v
                    for en, m in need_e.items():
                        if known.get(("E", en), -1) >= m:
                            continue
                        known[("E", en)] = m
                        e.wait_ge(esem[en][m % R], m // R + 1)
                    for k, v in need_c.items():
                        if known.get(k, 0) >= v:
                            continue
                        known[k] = v
                        e.wait_ge(csem[k[1]], v)
                    ins = o.fn(e)
                    if o.needs_inc:
                        if o.is_dma:
                            ins.then_inc(csem[o.tok[0][1]], 16)
                        else:
                            ins.then_inc(esem[o.eng][o.tok[0][2]], 1)
                if e_name == "sp":
                    for chan, groups in self.chan_groups.items():
                        v = groups[-1].val
                        if known.get(("c", chan), 0) < v:
                            e.wait_ge(csem[chan], v)

            @block.tensor
            def _(e):
                run("pe", e)

            @block.scalar
            def _(e):
                run("act", e)

            @block.vector
            def _(e):
                run("dve", e)

            @block.gpsimd
            def _(e):
                run("pool", e)

            @block.sync
            def _(e):
                run("sp", e)


def col_tiles(lo, hi):
    out = []
    c = lo
    while c < hi:
        n = hi - c
        if n > 512:
            n = 384
        out.append((c, n))
        c += n
    return out


def blk_keys(prefix, lo, n):
    return [(prefix, b) for b in range(lo // 128, (lo + n - 1) // 128 + 1)]


def build_program():
    nc = bass.Bass("TRN2", target_bir_lowering=False)
    p = Prog()

    def din(name, shape, dt=F32):
        return nc.dram_tensor(name, list(shape), dt, kind="ExternalInput").ap()

    def dout(name, shape, dt=F32):
        return nc.dram_tensor(name, list(shape), dt, kind="ExternalOutput").ap()

    def dint(name, shape, dt=F32):
        return nc.dram_tensor(name, list(shape), dt, kind="Internal").ap()

    x_in = din("x_in", [NT, D])
    cT_in = din("cT", [128, NJ * 32])
    ck_in = din("ck", [DEPTH, 4, 128, 256])
    cv_in = din("cv", [DEPTH, 4, 128, 256])
    w_ada = din("w_ada", [DEPTH, 96, 128, 4096])
    b_ada = din("b_ada", [DEPTH, 12288])
    w_in = din("w_in", [DEPTH, 148, 128, 4096])
    w_pab = din("w_pab", [DEPTH, 32, 128, 4096])
    w_o = din("w_o", [DEPTH, 32, 128, 4096])
    sinks_in = din("sinks", [DEPTH, 128, 16])
    sgu_g = din("sgu_g", [DEPTH, 2048])
    sgu_b = din("sgu_b", [DEPTH, 2048])
    w_s = din("w_s", [DEPTH, 8, 128, 128])
    b_s = din("b_s", [DEPTH, 8, 128])
    ln_g = din("ln_g", [DEPTH, D])
    ln_b = din("ln_b", [DEPTH, D])
    identf_in = din("identf", [128, 128])
    maskA_in = din("maskA", [128, 256])
    mask01_in = din("mask01", [128, 256])
    tril_in = din("tril", [128, 128])
    smask_in = din("smask", [128, 2, 128])

    y_p = dout("y_p", [1024, D])
    y_s = dout("y_s", [NS, D])
    o_wk = dout("o_wk", [DEPTH, 128, 256])
    o_wv = dout("o_wv", [DEPTH, 128, 256])
    o_nk = dout("o_nk", [DEPTH, NS, 256])
    o_nv = dout("o_nv", [DEPTH, NS, 256])
    o_sv = dout("o_sv", [DEPTH, NS, 2048])

    modd = dint("modd", [DEPTH, 32, 12288])
    spA = dint("spA", [16, 128, NM], BF16)
    spM = dint("spM", [32, 128, NM], BF16)
    spR = dint("spR", [NT, D])
    spX = dint("spX", [NT, D])

    dbg = {}
    if DEBUG:
        dbg["hT"] = dout("dbg_hT", [128, NJ * NM], BF16)
        dbg["A"] = dout("dbg_A", [16, 128, NM], BF16)
        dbg["B"] = dout("dbg_B", [128, 16 * NM], BF16)
        dbg["M"] = dout("dbg_M", [32, 128, NM], BF16)
        dbg["R"] = dout("dbg_R", [NT, D])

    off = [16512]

    def sb_at(name, shape, dt, offset):
        return nc.alloc_sbuf_tensor_at(name, list(shape), dt, offset=offset)

    def sb(name, shape, dt):
        nb = int(np.prod(shape[1:])) * (4 if dt == F32 else 2)
        o = off[0]
        off[0] = (o + nb + 63) // 64 * 64
        return sb_at(name, shape, dt, o)

    wr = sb("wr", [128, 4, NJ, 128], BF16)
    identf = sb("identf", [128, 128], F32)
    identb = sb("identb", [128, 128], BF16)
    maskA = sb("maskA", [128, 256], BF16)
    mask01 = sb("mask01", [128, 256], BF16)
    trilf = sb("trilf", [128, 128], F32)
    onesp = sb("onesp", [128, 2, 128], BF16)
    ones1 = sb("ones1", [1, 128], BF16)
    esink = sb("esink", [128, 16], F32)
    stats = sb("stats", [128, 11, 2], F32)
    smallf = sb("smallf", [128, 64], F32)
    scT = sb("scT", [128, NJ, 32], BF16)
    mod_bt = sb("mod_bt", [32, 512], F32)
    mod_mt = sb("mod_mt", [32, 512], F32)
    smask = sb("smask", [128, 2, 128], BF16)
    s_po = sb("s_po", [NS, 32], BF16)
    s_pc = sb("s_pc", [128, 32], BF16)
    hTm = sb("hTm", [128, NJ, NM], BF16)
    ZH = off[0]
    hTh = sb("hTh", [128, NJ, 128], BF16)
    Z0 = off[0]
    ZEND = 229376 - 128
    assert ZEND - Z0 > 86000, (Z0,)
    PZ = ZEND - 7168
    WmT = sb_at("WmT", [128, 8, 128], BF16, PZ)
    WsT = sb_at("WsT", [16, 8, 16], BF16, PZ + 2048)
    bsr_hi = sb_at("bsr_hi", [1, 8, 128], BF16, PZ + 2304)
    bsr_lo = sb_at("bsr_lo", [1, 8, 128], BF16, PZ + 4352)
    bss_hi = sb_at("bss_hi", [1, 8, 16], BF16, PZ + 6400)
    bss_lo = sb_at("bss_lo", [1, 8, 16], BF16, PZ + 6656)

    class Zone:
        def __init__(self, base):
            self.o = base

        def sb(self, name, shape, dt):
            nb = int(np.prod(shape[1:])) * (4 if dt == F32 else 2)
            o = self.o
            self.o = (o + nb + 63) // 64 * 64
            assert self.o <= ZEND, (name, self.o)
            return sb_at(name, shape, dt, o)

    ps = [nc.alloc_psum_tensor(f"ps{i}", [128, 512], F32) for i in range(8)]

    uid = [0]

    def U(s):
        uid[0] += 1
        return f"{s}_{uid[0]}"

    ring_n = [0]

    def ring_load(src):
        s = ring_n[0] % 4
        ring_n[0] += 1
        p.dma("pool", wr[:, s, :, :], src.rearrange("p (j c) -> p j c", c=128), writes=[("ring", s)], chan=f"ring{s}")
        return s

    def ring_pair_load(src2):
        if ring_n[0] % 2:
            ring_n[0] += 1
        s = ring_n[0] % 4
        ring_n[0] += 2
        p.dma("pool", wr[:, s:s + 2, :, :], src2.rearrange("n p (j c) -> p n j c", c=128),
              writes=[("ring", s), ("ring", s + 1)], chan=f"ring{s}")
        return s

    def mm(out, lhsT, rhs, start, stop, reads, writes):
        p.add("pe", lambda e: e.matmul(out, lhsT=lhsT, rhs=rhs, start=start, stop=stop), reads, writes)

    barrier_n = [0]
    bar_t = sb_at("bar_t", [128, 8], F32, ZEND)
    bar_b = sb_at("bar_b", [128, 8], BF16, ZEND + 64)

    def barrier():
        n = barrier_n[0]
        barrier_n[0] += 1
        p.add("pe", lambda e: e.matmul(ps[7][0:8, 0:8], lhsT=bar_b[:, 0:8], rhs=bar_b[:, 0:8], start=True, stop=True),
              reads=[("barb",)], writes=[("barA", n, "pe"), ("ps", 7)])
        p.add("act", lambda e: e.activation(bar_t[:, 0:1], bar_t[:, 4:5], AF.Copy), reads=[("bart",)], writes=[("barA", n, "act"), ("bt", "act")])
        p.add("dve", lambda e: e.tensor_copy(bar_t[:, 1:2], bar_t[:, 5:6]), reads=[("bart",)], writes=[("barA", n, "dve"), ("bt", "dve")])
        pend = list(p.pending)
        p.pending = []
        p.add("sp", lambda e: e.nop(), writes=[("barA", n, "sp")], extra_deps=pend)
        allk = [("barA", n, x) for x in ("pe", "act", "dve", "sp")]
        p.add("pe", lambda e: e.matmul(ps[7][0:8, 0:8], lhsT=bar_b[:, 0:8], rhs=bar_b[:, 0:8], start=True, stop=True),
              reads=allk + [("barb",)], writes=[("ps", 7)])
        p.add("act", lambda e: e.activation(bar_t[:, 0:1], bar_t[:, 4:5], AF.Copy), reads=allk + [("bart",)], writes=[("bt", "act")])
        p.add("dve", lambda e: e.tensor_copy(bar_t[:, 1:2], bar_t[:, 5:6]), reads=allk + [("bart",)], writes=[("bt", "dve")])
        p.add("sp", lambda e: e.nop(), reads=allk)

    z = Zone(Z0)
    tmpf = z.sb("c_tmpf", [128, 1024], F32)
    p.add("dve", lambda e: e.memset(bar_t[:], 0.0), writes=[("bart",)])
    p.add("dve", lambda e: e.memset(bar_b[:], 0.0), writes=[("barb",)])
    g0 = p.dma_group("c0")
    p.dma("sp", identf[:], identf_in, writes=["identf"], group=g0)
    p.dma("sp", trilf[:], tril_in, writes=["trilf"], group=g0)
    p.dma("sp", tmpf[:, 0:256], maskA_in, writes=["c_t0"], group=g0)
    p.dma("sp", tmpf[:, 256:512], mask01_in, writes=["c_t1"], group=g0)
    p.add("dve", lambda e: e.tensor_copy(identb[:], identf[:]), reads=["identf"], writes=["identb"])
    p.add("dve", lambda e: e.tensor_copy(maskA[:], tmpf[:, 0:256]), reads=["c_t0"], writes=["maskA"])
    p.add("dve", lambda e: e.tensor_copy(mask01[:], tmpf[:, 256:512]), reads=["c_t1"], writes=["mask01"])
    p.add("dve", lambda e: e.memset(onesp[:], 0.0), writes=["onesp"])
    p.add("dve", lambda e: e.memset(onesp[:, 0, 0:64], 1.0), writes=["onesp"])
    p.add("dve", lambda e: e.memset(onesp[:, 1, 64:128], 1.0), writes=["onesp"])
    p.add("dve", lambda e: e.memset(ones1[:], 1.0), writes=["ones1"])

    def mod_setup():
        zc = Zone(Z0)
        cTf = zc.sb("c_cTf", [128, NJ * 32], F32)
        p.dma("sp", cTf[:], cT_in, writes=["cTf"], chan="c1")
        p.add("act", lambda e: e.activation(scT[:].rearrange("p j m -> p (j m)"), cTf[:], AF.Silu), reads=["cTf"], writes=["scT"])

    def mod_group(l, grp, bank):
        pk = ("ps", bank)
        p.dma("sp", mod_bt[:], b_ada[l:l + 1, grp * 512:(grp + 1) * 512].broadcast_to([32, 512]), writes=[("mod_bt",)], chan="mbt")
        for q in range(2):
            nb = grp * 4 + q * 2
            s = ring_pair_load(w_ada[l, nb:nb + 2])
            for j in range(NJ):
                mm(ps[bank][0:32, q * 256:(q + 1) * 256].rearrange("p (a c) -> p a c", c=128), scT[:, j, :], wr[:, s:s + 2, j, :], j == 0, j == NJ - 1,
                   reads=[("ring", s), ("ring", s + 1), "scT"], writes=[pk])
        p.add("dve", lambda e: e.tensor_tensor(out=mod_mt[:], in0=ps[bank][0:32, :], in1=mod_bt[:], op=ALU.add),
              reads=[pk, ("mod_bt",)], writes=[("mod_mt",)])
        p.dma("sp", modd[l, :, grp * 512:(grp + 1) * 512], mod_mt[:], reads=[("mod_mt",)], writes=[("modd", l)], chan="mmo")

    def hT_dst(blk, j0, nj):
        if blk == 0:
            return hTh[:, j0:j0 + nj, :]
        if blk == 10:
            return hTm[:, j0:j0 + nj, 1152:1168]
        return hTm[:, j0:j0 + nj, (blk - 1) * 128:blk * 128]

    def phase_h(l, src, blocks, do_ln, xdst):
        zz = Zone(Z0)
        H = 2048
        last = l >= DEPTH
        if not last:
            s1 = zz.sb(U("s1"), [128, H], F32)
            sh = zz.sb(U("sh"), [128, H], F32)
            s1s = zz.sb(U("s1s"), [NS, H], F32)
            shs = zz.sb(U("shs"), [NS, H], F32)
        if do_ln:
            gg = zz.sb(U("gg"), [128, H], F32)
            bb = zz.sb(U("bb"), [128, H], F32)
        xt = [zz.sb(U("xt"), [128, H], F32) for _ in range(2)]
        tt = [zz.sb(U("tt"), [128, H], F32) for _ in range(2)]
        hb_ = zz.sb(U("hb"), [128, H], BF16)
        hb = [hb_, hb_]
        tag = U("ph")
        if do_ln:
            rt = [xt[0], tt[0]]
            rtk = [(tag, "xt", 0), (tag, "tt", 0)]
            bst = zz.sb(U("bst"), [128, 8, 6], F32)
            mv = zz.sb(U("mv"), [128, 2], F32)
            for i, blk in enumerate(blocks):
                npart = NS if blk == 10 else 128
                r0 = blk * 128
                for hf in range(2):
                    p.dma("sp", rt[hf][0:npart, :], src[r0:r0 + npart, hf * H:(hf + 1) * H],
                          reads=[("spR",), ("spRs",)], writes=[rtk[hf]], chan=f"{tag}r{hf}")
                    for q in range(4):
                        p.add("dve", lambda e, hf=hf, q=q, npart=npart: e.bn_stats(bst[0:npart, hf * 4 + q, :], rt[hf][0:npart, q * 512:(q + 1) * 512]),
                              reads=[rtk[hf]], writes=[(tag, "bst")])
                p.add("dve", lambda e, npart=npart: e.bn_aggr(mv[0:npart, :], bst[0:npart, :, :]), reads=[(tag, "bst")], writes=[(tag, "mv")])
                p.add("dve", lambda e, npart=npart: e.tensor_scalar_add(mv[0:npart, 1:2], mv[0:npart, 1:2], LN_EPS), reads=[(tag, "mv")], writes=[(tag, "mv")])
                p.add("act", lambda e, npart=npart: e.activation(smallf[0:npart, 4:5], mv[0:npart, 1:2], AF.Sqrt), reads=[(tag, "mv")], writes=[(tag, "mvs")])
                p.add("dve", lambda e, blk=blk, npart=npart: e.reciprocal(stats[0:npart, blk, 0:1], smallf[0:npart, 4:5]),
                      reads=[(tag, "mvs")], writes=[("stats", blk)])
                p.add("dve", lambda e, blk=blk, npart=npart: e.scalar_tensor_tensor(out=stats[0:npart, blk, 1:2], in0=mv[0:npart, 0:1], scalar=-1.0, in1=stats[0:npart, blk, 0:1], op0=ALU.mult, op1=ALU.mult),
                      reads=[(tag, "mv"), ("stats", blk)], writes=[("stats", blk)])
        for hf in range(2):
            c0 = hf * H
            kh = (tag, "par", hf)
            g = p.dma_group(f"{tag}p{hf}")
            wk = [(tag, "s1"), (tag, "sh"), (tag, "s1s"), (tag, "shs"), (tag, "gg"), (tag, "bb")]
            if not last:
                p.dma("sp", s1[:], modd[l, 16:17, D + c0:D + c0 + H].broadcast_to([128, H]), reads=[("modd", l)], writes=[wk[0]], group=g)
                p.dma("sp", sh[:], modd[l, 16:17, c0:c0 + H].broadcast_to([128, H]), reads=[("modd", l)], writes=[wk[1]], group=g)
                p.dma("sp", s1s[:], modd[l, 0:NS, D + c0:D + c0 + H], reads=[("modd", l)], writes=[wk[2]], group=g)
                p.dma("sp", shs[:], modd[l, 0:NS, c0:c0 + H], reads=[("modd", l)], writes=[wk[3]], group=g)
            if do_ln:
                p.dma("sp", gg[:], ln_g[l - 1:l, c0:c0 + H].broadcast_to([128, H]), writes=[wk[4]], group=g)
                p.dma("sp", bb[:], ln_b[l - 1:l, c0:c0 + H].broadcast_to([128, H]), writes=[wk[5]], group=g)
            if not last:
                p.add("dve", lambda e: e.tensor_scalar_add(s1[:], s1[:], 1.0), reads=[wk[0]], writes=[wk[0]])
                p.add("dve", lambda e: e.tensor_scalar_add(s1s[:], s1s[:], 1.0), reads=[wk[2]], writes=[wk[2]])
            for i, blk in enumerate(blocks):
                b2 = i % 2
                npart = NS if blk == 10 else 128
                r0 = blk * 128
                kx, kt, kb_ = (tag, "xt", b2), (tag, "tt", b2), (tag, "hb", 0)
                p.dma("sp", xt[b2][0:npart, :], src[r0:r0 + npart, c0:c0 + H], reads=[("spR",), ("spRs",)], writes=[kx], chan=f"{tag}x{b2}")
                if do_ln:
                    p.add("act", lambda e, b2=b2, blk=blk, npart=npart: e.activation(xt[b2][0:npart, :], xt[b2][0:npart, :], AF.Identity,
                                                                             bias=stats[0:npart, blk, 1:2], scale=stats[0:npart, blk, 0:1]),
                          reads=[kx, ("stats", blk)], writes=[kx])
                    p.add("dve", lambda e, b2=b2, npart=npart: e.tensor_tensor(out=xt[b2][0:npart, :], in0=xt[b2][0:npart, :], in1=gg[0:npart, :], op=ALU.mult),
                          reads=[kx, wk[4]], writes=[kx])
                    p.add("dve", lambda e, b2=b2, npart=npart: e.tensor_tensor(out=xt[b2][0:npart, :], in0=xt[b2][0:npart, :], in1=bb[0:npart, :], op=ALU.add),
                          reads=[kx, wk[5]], writes=[kx])
                    dst = xdst(blk)
                    if dst is not None:
                        p.dma("sp", dst[:, c0:c0 + H], xt[b2][0:npart, :], reads=[kx], writes=[("xdst", blk)], chan=f"{tag}o{b2}")
                if last:
                    continue
                a1 = s1s if blk == 10 else s1
                a2 = shs if blk == 10 else sh
                k1 = wk[2] if blk == 10 else wk[0]
                k2 = wk[3] if blk == 10 else wk[1]
                p.add("dve", lambda e, b2=b2, npart=npart, a1=a1: e.tensor_tensor(out=tt[b2][0:npart, :], in0=xt[b2][0:npart, :], in1=a1[0:npart, :], op=ALU.mult),
                      reads=[kx, k1], writes=[kt])
                p.add("dve", lambda e, b2=b2, npart=npart, a2=a2: e.tensor_tensor(out=hb[b2][0:npart, :], in0=tt[b2][0:npart, :], in1=a2[0:npart, :], op=ALU.add),
                      reads=[kt, k2], writes=[kb_])
                for g8 in range(2):
                    bank = 4 + (2 * i + g8) % 4
                    pst = ps[bank][:, :].bitcast(BF16)
                    for q in range(8):
                        jj = g8 * 8 + q
                        p.add("pe", lambda e, b2=b2, jj=jj, q=q, pst=pst, npart=npart: e.transpose(pst[:, q * 128:q * 128 + npart], hb[b2][0:npart, jj * 128:(jj + 1) * 128], identb[0:npart, 0:npart]),
                              reads=[kb_, "identb"], writes=[("ps", bank)])
                    j0 = hf * 16 + g8 * 8
                    dst = hT_dst(blk, j0, 8)
                    eng = "act" if (g8 == 0) else "dve"
                    srcv = pst[:, :].rearrange("p (q c) -> p q c", c=128)[:, :, 0:npart]
                    if eng == "act":
                        p.add("act", lambda e, dst=dst, srcv=srcv: e.activation(dst, srcv, AF.Copy), reads=[("ps", bank)], writes=[("hT", blk)])
                    else:
                        p.add("dve", lambda e, dst=dst, srcv=srcv: e.tensor_copy(dst, srcv), reads=[("ps", bank)], writes=[("hT", blk)])

    def hT_tiles(lo, hi):
        out = []
        if lo < 128:
            out.append((lambda j: hTh[:, j, :], 0, 128))
            lo = 128
        for (c, n) in col_tiles(lo, hi):
            out.append((lambda j, c=c, n=n: hTm[:, j, c - 128:c - 128 + n], c, n))
        return out

    def proj_fm(l, nb_idx, lo, hi, banks):
        s = ring_load(w_in[l, nb_idx])
        res = []
        for ti, (fn, c, n) in enumerate(hT_tiles(lo, hi)):
            b = banks[ti]
            rk = [("ring", s)] + blk_keys("hT", c, n)
            for j in range(NJ):
                mm(ps[b][:, 0:n], wr[:, s, j, :], fn(j), j == 0, j == NJ - 1, reads=rk, writes=[("ps", b)])
            res.append((b, c, n))
        return res, s

    def phase_attn(l):
        zz = Zone(Z0)
        tag = U("at")
        kvlo = 128 * l
        flo = 128 * (l + 1)
        nkb = NPB - l
        kT = zz.sb(U("kT"), [128, 2, NT], BF16)
        vpad = zz.sb(U("vpad"), [128, 11, 2, 2, 128], BF16)
        qc = [zz.sb(U("qc"), [128, NT], BF16) for _ in range(2)]
        at = [zz.sb(U("atc"), [128, NT], BF16) for _ in range(2)]
        pt = [zz.sb(U("pt"), [128, 2, 256], BF16) for _ in range(3)]
        dtm = zz.sb(U("dtm"), [128, 512], F32)
        ot = [zz.sb(U("ot"), [128, 256], F32) for _ in range(2)]
        prep_sample(l, zz)
        p.add("dve", lambda e: e.memset(vpad[:].rearrange("p a b c d -> p (a b c d)"), 0.0), writes=[(tag, "vpad")])
        p.dma("sp", esink[:], sinks_in[l], writes=[(tag, "esink")], chan=U("es"))
        p.add("act", lambda e: e.activation(esink[:], esink[:], AF.Exp), reads=[(tag, "esink")], writes=[(tag, "esink")])
        for m in range(2):
            res, s = proj_fm(l, m, kvlo, NT, [0, 1, 2, 3])
            for (b, c, n) in res:
                p.add("act", lambda e, b=b, c=c, n=n, m=m: e.activation(kT[:, m, c:c + n], ps[b][:, 0:n], AF.Copy),
                      reads=[("ps", b)], writes=[(tag, "kT", m)])
            for (blk, npart, bank, dstap) in ((9, 128, 4, o_wk[l, :, m * 128:(m + 1) * 128]), (10, NS, 5, o_nk[l, :, m * 128:(m + 1) * 128])):
                for j in range(NJ):
                    lh = hT_dst(blk, j, 1)
                    mm(ps[bank][0:npart, 0:128], lh.rearrange("p a c -> p (a c)"), wr[:, s, j, :], j == 0, j == NJ - 1,
                       reads=[("ring", s), ("hT", blk)], writes=[("ps", bank)])
                o = ot[0 if blk == 9 else 1]
                ko = (tag, "ot", blk)
                p.add("dve", lambda e, o=o, bank=bank, npart=npart: e.tensor_copy(o[0:npart, 0:128], ps[bank][0:npart, 0:128]), reads=[("ps", bank)], writes=[ko])
                p.dma("sp", dstap, o[0:npart, 0:128], reads=[ko], chan=U("ok"))
        vblocks = list(range(l, NPB)) + [10]
        for m in range(2):
            s = ring_load(w_in[l, 2 + m])
            for i, blk in enumerate(vblocks):
                npart = NS if blk == 10 else 128
                bank = i // 4
                q = i % 4
                for j in range(NJ):
                    lh = hT_dst(blk, j, 1)
                    mm(ps[bank][0:npart, q * 128:(q + 1) * 128], lh.rearrange("p a c -> p (a c)"), wr[:, s, j, :], j == 0, j == NJ - 1,
                       reads=[("ring", s), ("hT", blk)], writes=[("ps", bank)])
            for i, blk in enumerate(vblocks):
                npart = NS if blk == 10 else 128
                bank = i // 4
                q = i % 4
                for hh in range(2):
                    p.add("act", lambda e, blk=blk, npart=npart, bank=bank, q=q, hh=hh, m=m: e.activation(
                        vpad[0:npart, blk, m, hh, hh * 64:hh * 64 + 64], ps[bank][0:npart, q * 128 + hh * 64:q * 128 + hh * 64 + 64], AF.Copy),
                        reads=[("ps", bank), (tag, "vpad")], writes=[(tag, "vpad", blk)])
                if blk in (9, 10):
                    o = ot[0 if blk == 9 else 1]
                    ko = (tag, "ot", blk)
                    dstap = (o_wv if blk == 9 else o_nv)[l, :, m * 128:(m + 1) * 128]
                    p.add("dve", lambda e, o=o, bank=bank, npart=npart, q=q: e.tensor_copy(o[0:npart, 128:256], ps[bank][0:npart, q * 128:(q + 1) * 128]),
                          reads=[("ps", bank)], writes=[ko])
                    p.dma("sp", dstap, o[0:npart, 128:256], reads=[ko], chan=U("ov"))
        for c in range(16):
            m = c // 8
            b2 = c % 2
            kq, ka = (tag, "qc", b2), (tag, "at", b2)
            res, _ = proj_fm(l, 4 + c, flo, NT, [0, 1, 2])
            for (b, c0, n) in res:
                p.add("dve", lambda e, b=b, c0=c0, n=n, b2=b2: e.tensor_copy(qc[b2][:, c0:c0 + n], ps[b][:, 0:n]), reads=[("ps", b)], writes=[kq])
            res, _ = proj_fm(l, 20 + c, flo, NT, [3, 4, 5])
            for (b, c0, n) in res:
                p.add("act", lambda e, b=b, c0=c0, n=n, b2=b2: e.activation(at[b2][:, c0:c0 + n], ps[b][:, 0:n], AF.Silu), reads=[("ps", b)], writes=[ka])
            def acol(cabs):
                r = cabs - flo
                return r // 512, r % 512
            prev = None
            pi = 0
            first_q = {}
            for kb in range(l, NPB):
                qbs = [qb for qb in (kb, kb + 1) if l + 1 <= qb <= NPB - 1]
                if not qbs:
                    continue
                qlo = qbs[0] * 128
                nq = 128 * len(qbs)
                mk = mask01 if kb <= 1 else maskA
                mk_key = "mask01" if kb <= 1 else "maskA"
                moff = 0 if qbs[0] == kb else 128
                cur = []
                for hh in range(2):
                    sb_ = 6 + (pi % 2)
                    half = ((pi // 2) % 2) * 256
                    pi += 1
                    pr = slice(hh * 64, hh * 64 + 64)
                    mm(ps[sb_][:, half:half + nq], kT[pr, m, kb * 128:(kb + 1) * 128], qc[b2][pr, qlo:qlo + nq], True, False,
                       reads=[(tag, "kT", m), kq], writes=[("ps", sb_)])
                    mm(ps[sb_][:, half:half + nq], identb[:, :], mk[:, moff:moff + nq], False, True,
                       reads=["identb", mk_key], writes=[("ps", sb_)])
                    ptile = pt[(kb * 2 + hh) % 3] if False else None
                    cur.append((sb_, half))
                pti = kb % 3
                for hh in range(2):
                    sb_, half = cur[hh]
                    p.add("act", lambda e, pti=pti, hh=hh, sb_=sb_, half=half, nq=nq: e.activation(pt[pti][:, hh, 0:nq], ps[sb_][:, half:half + nq], AF.Exp, scale=0.125),
                          reads=[("ps", sb_)], writes=[(tag, "pt", pti)])
                for qi, qb in enumerate(qbs):
                    bk, co = acol(qb * 128)
                    is_first = qb not in first_q
                    first_q[qb] = True
                    is_last = (qb == kb)
                    po = qi * 128
                    for hh in range(2):
                        st_ = is_first and hh == 0
                        sp_ = is_last and hh == 1
                        mm(ps[bk][:, co:co + 128], vpad[:, kb, m, hh, :], pt[pti][:, hh, po:po + 128], st_, sp_,
                           reads=[(tag, "vpad", kb), (tag, "pt", pti)], writes=[("ps", bk)])
                        mm(ps[3 + bk][:, co:co + 128], onesp[:, hh, :], pt[pti][:, hh, po:po + 128], st_, sp_,
                           reads=["onesp", (tag, "pt", pti)], writes=[("ps", 3 + bk)])
            bk, co = acol(NPB * 128)
            sample_attn(l, c, m, b2, tag, kT, vpad, qc, pt, bk, co, zz)
            ncols = NT - flo
            for t0 in range(0, ncols, 512):
                n = min(512, ncols - t0)
                bk = t0 // 512
                ca = flo + t0
                p.add("dve", lambda e, bk=bk, n=n, c=c: e.tensor_scalar(dtm[:, 0:n], ps[3 + bk][:, 0:n], esink[:, c:c + 1], None, op0=ALU.add),
                      reads=[("ps", 3 + bk), (tag, "esink")], writes=[(tag, "dtm")])
                p.add("dve", lambda e, n=n: e.reciprocal(dtm[:, 0:n], dtm[:, 0:n]), reads=[(tag, "dtm")], writes=[(tag, "dtm")])
                p.add("dve", lambda e, bk=bk, n=n: e.tensor_tensor(out=dtm[:, 0:n], in0=ps[bk][:, 0:n], in1=dtm[:, 0:n], op=ALU.mult),
                      reads=[("ps", bk), (tag, "dtm")], writes=[(tag, "dtm")])
                p.add("dve", lambda e, n=n, ca=ca, b2=b2: e.tensor_tensor(out=at[b2][:, ca:ca + n], in0=dtm[:, 0:n], in1=at[b2][:, ca:ca + n], op=ALU.mult),
                      reads=[(tag, "dtm"), ka], writes=[ka])
            p.dma("sp", spA[c, :, flo - 128:NM], at[b2][:, flo:NT], reads=[ka], writes=[("spA", c)], chan=f"{tag}sa{b2}")
            if DEBUG and l == 0:
                p.dma("sp", dbg["A"][c, :, :], at[b2][:, 128:NT], reads=[ka], chan=U("dbgA"))

    def sample_attn(l, c, m, b2, tag, kT, vpad, qc, pt, bk, co, zz):
        st = S_at[l]
        kq = (tag, "qc", b2)
        scol = NPB * 128
        for hh in range(2):
            pr = slice(hh * 64, hh * 64 + 64)
            mm(ps[7][0:NS, 256 + hh * 16:256 + hh * 16 + NS], kT[pr, m, scol:scol + NS], qc[b2][pr, scol:scol + NS], True, False,
               reads=[(tag, "kT", m), kq], writes=[("ps", 7)])
            mm(ps[7][0:NS, 256 + hh * 16:256 + hh * 16 + NS], identb[0:NS, 0:NS], st["smask"][0:NS, 1, 0:NS], False, True,
               reads=["identb", "smask"], writes=[("ps", 7)])
        p.add("act", lambda e: e.activation(st["po"][0:NS, 0:32], ps[7][0:NS, 256:288], AF.Exp, scale=0.125),
              reads=[("ps", 7)], writes=[(tag, "po")])
        for sq in range(4):
            for hh in range(2):
                pr = slice(hh * 64, hh * 64 + 64)
                cc = 320 + hh * 16 + sq * 4
                mm(ps[7][:, cc:cc + 4], st["ckT"][pr, sq, m, :], qc[b2][pr, scol + sq * 4:scol + sq * 4 + 4], True, False,
                   reads=[("ckT", l), kq], writes=[("ps", 7)])
                mm(ps[7][:, cc:cc + 4], identb[:, :], st["smask"][:, 0, 0:4], False, True,
                   reads=["identb", "smask"], writes=[("ps", 7)])
        p.add("act", lambda e: e.activation(st["pc"][:, 0:32], ps[7][:, 320:352], AF.Exp, scale=0.125),
              reads=[("ps", 7)], writes=[(tag, "pc")])
        for hh in range(2):
            mm(ps[bk][:, co:co + NS], vpad[0:NS, 10, m, hh, :], st["po"][0:NS, hh * 16:hh * 16 + NS], hh == 0, False,
               reads=[(tag, "vpad", 10), (tag, "po")], writes=[("ps", bk)])
            mm(ps[3 + bk][:, co:co + NS], onesp[0:NS, hh, :], st["po"][0:NS, hh * 16:hh * 16 + NS], hh == 0, False,
               reads=["onesp", (tag, "po")], writes=[("ps", 3 + bk)])
        for sq in range(4):
            for hh in range(2):
                lastmm = (sq == 3 and hh == 1)
                cc = hh * 16 + sq * 4
                mm(ps[bk][:, co + sq * 4:co + sq * 4 + 4], st["cvp"][:, sq, m, hh, :], st["pc"][:, cc:cc + 4], False, lastmm,
                   reads=[("cvp", l), (tag, "pc")], writes=[("ps", bk)])
                mm(ps[3 + bk][:, co + sq * 4:co + sq * 4 + 4], onesp[:, hh, :], st["pc"][:, cc:cc + 4], False, lastmm,
                   reads=["onesp", (tag, "pc")], writes=[("ps", 3 + bk)])

    S_at = {}

    def prep_sample(l, zz):
        tag = U("ps")
        s_ckT = zz.sb(U("s_ckT"), [128, 4, 2, 128], BF16)
        s_cvp = zz.sb(U("s_cvp"), [128, 4, 2, 2, 128], BF16)
        cf = zz.sb(U("cf"), [128, 4, 256], F32)
        vf = zz.sb(U("vf"), [128, 4, 256], F32)
        S_at[l] = dict(smask=smask, po=s_po, pc=s_pc, ckT=s_ckT, cvp=s_cvp)
        p.dma("sp", cf[:], ck_in[l].rearrange("s j f -> j s f"), writes=[(tag, "cf")], chan=U("ck"))
        p.dma("sp", vf[:], cv_in[l].rearrange("s j f -> j s f"), writes=[(tag, "vf")], chan=U("cv"))
        p.add("dve", lambda e: e.memset(s_cvp[:].rearrange("p a b c d -> p (a b c d)"), 0.0), writes=[("cvp", l)])
        for sq in range(4):
            for m in range(2):
                for hh in range(2):
                    p.add("dve", lambda e, sq=sq, m=m, hh=hh: e.tensor_copy(s_cvp[:, sq, m, hh, hh * 64:hh * 64 + 64], vf[:, sq, m * 128 + hh * 64:m * 128 + hh * 64 + 64]),
                          reads=[(tag, "vf")], writes=[("cvp", l)])
                p.add("pe", lambda e, sq=sq, m=m: e.transpose(ps[6][:, 0:128], cf[:, sq, m * 128:(m + 1) * 128], identf[:, :]),
                      reads=[(tag, "cf"), "identf"], writes=[("ps", 6)])
                p.add("dve", lambda e, sq=sq, m=m: e.tensor_copy(s_ckT[:, sq, m, :], ps[6][:, 0:128]), reads=[("ps", 6)], writes=[("ckT", l)])


    def phase_sgu(l):
        tag = U("sg")
        flo = 128 * (l + 1)
        fblocks = list(range(l + 1, NPB)) + [10]
        nfb = len(fblocks)
        vbnS = sb_at(U("vbn"), [128, 10, 2048], BF16, ZH)
        raw = sb_at(U("raw"), [128, 4, 2048], F32, ZH + 40960)
        z2 = Zone(ZH + 40960 + 32768)
        tmp = z2.sb(U("tmp"), [128, 1024], F32)
        sgt = z2.sb(U("sgt"), [128, 1024], F32)
        sbt = z2.sb(U("sbt"), [128, 1024], F32)
        bst = z2.sb(U("bst"), [128, 4, 6], F32)
        mv = z2.sb(U("mv"), [128, 10, 2], F32)
        zr = Zone(ZH + 40960)
        wsf = zr.sb(U("wsf"), [128, 128], F32)
        bsr_f = zr.sb(U("bsr_f"), [1, 8, 128], F32)
        bsr_t = zr.sb(U("bsr_t"), [1, 8, 128], F32)
        bss_f = zr.sb(U("bss_f"), [1, 8, 16], F32)
        bss_t = zr.sb(U("bss_t"), [1, 8, 16], F32)
        assert z2.o <= PZ, z2.o
        for g in range(8):
            p.dma("sp", wsf[:], w_s[l, g], writes=[(tag, "wsf")], chan=U("ws"))
            p.add("dve", lambda e: e.tensor_tensor(out=wsf[:], in0=wsf[:], in1=trilf[:], op=ALU.mult), reads=[(tag, "wsf"), "trilf"], writes=[(tag, "wsf")])
            p.add("pe", lambda e: e.transpose(ps[6][:, 0:128], wsf[:, :], identf[:, :]), reads=[(tag, "wsf"), "identf"], writes=[("ps", 6)])
            p.add("dve", lambda e, g=g: e.tensor_copy(WmT[:, g, :], ps[6][:, 0:128]), reads=[("ps", 6)], writes=[(tag, "WmT")])
        p.add("dve", lambda e: e.memset(WsT[:].rearrange("p g t -> p (g t)"), 0.0), writes=[(tag, "WsT")])
        gws = p.dma_group(U("wst"))
        for sq in range(4):
            p.dma("sp", WsT[sq * 4:sq * 4 + 4, :, sq * 4:sq * 4 + 4], WmT[0:4, :, 0:4], reads=[(tag, "WmT"), (tag, "WsT")], writes=[(tag, "WsT", sq)], group=gws,
                  allow_slow_non_contiguous=True)
        gb = p.dma_group(U("bs"))
        p.dma("sp", bsr_f[0:1, :, :], b_s[l:l + 1, :, :], writes=[(tag, "bsrf")], group=gb)
        for sq in range(4):
            p.dma("sp", bss_f[0:1, :, sq * 4:sq * 4 + 4], b_s[l:l + 1, :, 0:4], writes=[(tag, "bssf", sq)], group=gb, allow_slow_non_contiguous=True)
        for (f_, hi_, lo_, t_, kk) in ((bsr_f, bsr_hi, bsr_lo, bsr_t, "bsrf"), (bss_f, bss_hi, bss_lo, bss_t, "bssf")):
            rk_ = [(tag, kk)] + [(tag, kk, x) for x in range(4)]
            p.add("dve", lambda e, f_=f_, hi_=hi_: e.tensor_copy(hi_[:], f_[:]), reads=rk_, writes=[(tag, kk, "hi")])
            p.add("dve", lambda e, hi_=hi_, t_=t_: e.tensor_copy(t_[:], hi_[:]), reads=[(tag, kk, "hi")], writes=[(tag, kk, "t")])
            p.add("dve", lambda e, f_=f_, t_=t_: e.tensor_tensor(out=t_[:], in0=f_[:], in1=t_[:], op=ALU.subtract), reads=rk_ + [(tag, kk, "t")], writes=[(tag, kk, "t")])
            p.add("dve", lambda e, lo_=lo_, t_=t_: e.tensor_copy(lo_[:], t_[:]), reads=[(tag, kk, "t")], writes=[(tag, kk, "lo")])
        barrier()
        for pas in range(3):
            pblocks = list(enumerate(fblocks))[pas * 4:(pas + 1) * 4]
            if not pblocks:
                continue
            for i8 in range(8):
                s = ring_pair_load(w_in[l, 36 + 2 * i8:36 + 2 * i8 + 2])
                pb0 = (i8 % 2) * 2
                for li, (i, blk) in enumerate(pblocks):
                    npart = NS if blk == 10 else 128
                    bank = pb0 + li // 2
                    q = li % 2
                    for j in range(NJ):
                        lh = hT_dst(blk, j, 1)
                        mm(ps[bank][0:npart, q * 256:(q + 1) * 256].rearrange("p (a c) -> p a c", c=128), lh.rearrange("p a c -> p (a c)"), wr[:, s:s + 2, j, :], j == 0, j == NJ - 1,
                           reads=[("ring", s), ("ring", s + 1), ("hT", blk)], writes=[("ps", bank)])
                for li, (i, blk) in enumerate(pblocks):
                    npart = NS if blk == 10 else 128
                    bank = pb0 + li // 2
                    q = li % 2
                    if li % 2 == 0:
                        p.add("act", lambda e, li=li, q=q, bank=bank, i8=i8, npart=npart: e.activation(raw[0:npart, li, i8 * 256:(i8 + 1) * 256], ps[bank][0:npart, q * 256:(q + 1) * 256], AF.Copy),
                              reads=[("ps", bank)], writes=[(tag, "raw", li)])
                    else:
                        p.add("dve", lambda e, li=li, q=q, bank=bank, i8=i8, npart=npart: e.tensor_copy(raw[0:npart, li, i8 * 256:(i8 + 1) * 256], ps[bank][0:npart, q * 256:(q + 1) * 256]),
                              reads=[("ps", bank)], writes=[(tag, "raw", li)])
            for li, (i, blk) in enumerate(pblocks):
                npart = NS if blk == 10 else 128
                for q in range(4):
                    p.add("dve", lambda e, li=li, q=q, npart=npart: e.bn_stats(bst[0:npart, q, :], raw[0:npart, li, q * 512:(q + 1) * 512]),
                          reads=[(tag, "raw", li)], writes=[(tag, "bst")])
                p.add("dve", lambda e, npart=npart: e.bn_aggr(smallf[0:npart, 0:2], bst[0:npart, :, :]), reads=[(tag, "bst")], writes=[(tag, "mvt")])
                p.add("dve", lambda e, npart=npart: e.tensor_scalar_add(smallf[0:npart, 1:2], smallf[0:npart, 1:2], LN_EPS), reads=[(tag, "mvt")], writes=[(tag, "mvt")])
                p.add("act", lambda e, npart=npart: e.activation(smallf[0:npart, 2:3], smallf[0:npart, 1:2], AF.Sqrt), reads=[(tag, "mvt")], writes=[(tag, "mvs")])
                p.add("dve", lambda e, i=i, npart=npart: e.reciprocal(mv[0:npart, i, 1:2], smallf[0:npart, 2:3]),
                      reads=[(tag, "mvs")], writes=[(tag, "mv", i)])
                p.add("dve", lambda e, i=i, npart=npart: e.tensor_copy(mv[0:npart, i, 0:1], smallf[0:npart, 0:1]), reads=[(tag, "mvt")], writes=[(tag, "mv", i)])
            for hf in range(2):
                c0 = hf * 1024
                g = p.dma_group(U("sgp"))
                p.dma("sp", sgt[:], sgu_g[l:l + 1, c0:c0 + 1024].broadcast_to([128, 1024]), writes=[(tag, "sgt")], group=g)
                p.dma("sp", sbt[:], sgu_b[l:l + 1, c0:c0 + 1024].broadcast_to([128, 1024]), writes=[(tag, "sbt")], group=g)
                for li, (i, blk) in enumerate(pblocks):
                    npart = NS if blk == 10 else 128
                    p.add("dve", lambda e, li=li, i=i, npart=npart, c0=c0: e.tensor_scalar(tmp[0:npart, :], raw[0:npart, li, c0:c0 + 1024], mv[0:npart, i, 0:1], mv[0:npart, i, 1:2], op0=ALU.subtract, op1=ALU.mult),
                          reads=[(tag, "raw", li), (tag, "mv", i)], writes=[(tag, "tmp")])
                    p.add("dve", lambda e, npart=npart: e.tensor_tensor(out=tmp[0:npart, :], in0=tmp[0:npart, :], in1=sgt[0:npart, :], op=ALU.mult),
                          reads=[(tag, "tmp"), (tag, "sgt")], writes=[(tag, "tmp")])
                    if blk == 10:
                        p.add("dve", lambda e, npart=npart: e.tensor_tensor(out=tmp[0:npart, :], in0=tmp[0:npart, :], in1=sbt[0:npart, :], op=ALU.add),
                              reads=[(tag, "tmp"), (tag, "sbt")], writes=[(tag, "tmp")])
                        p.dma("sp", o_sv[l, :, c0:c0 + 1024], tmp[0:NS, :], reads=[(tag, "tmp")], writes=[(tag, "osv")], chan=U("osv"))
                        p.add("dve", lambda e, i=i, npart=npart, c0=c0: e.tensor_copy(vbnS[0:npart, i, c0:c0 + 1024], tmp[0:npart, :]),
                              reads=[(tag, "tmp")], writes=[(tag, "vbn", i)])
                    else:
                        p.add("dve", lambda e, i=i, npart=npart, c0=c0: e.tensor_tensor(out=vbnS[0:npart, i, c0:c0 + 1024], in0=tmp[0:npart, :], in1=sbt[0:npart, :], op=ALU.add),
                              reads=[(tag, "tmp"), (tag, "sbt")], writes=[(tag, "vbn", i)])
        barrier()
        BT = sb_at("BT_%d" % l, [128, 16, NM], BF16, ZH + 40960)
        z3 = Zone(ZH + 40960 + 37376)
        ut = z3.sb(U("ut"), [128, NM], F32)
        szb = z3.sb(U("szb"), [128, NM], F32)
        ncols = NT - flo
        for j16 in range(16):
            g8 = j16 // 2
            if l == 0 and j16 % 2 == 0:
                mod_group(0, 16 + j16 // 2, 6 + (j16 // 2) % 2)
            res, _ = proj_fm(l, 68 + j16, flo, NT, [3, 4, 5])
            for (b, c0, n) in res:
                p.add("act", lambda e, b=b, c0=c0, n=n: e.activation(szb[:, c0 - flo:c0 - flo + n], ps[b][:, 0:n], AF.Silu), reads=[("ps", b)], writes=[(tag, "szb")])
            res, _ = proj_fm(l, 52 + j16, flo, NT, [0, 1, 2])
            for (b, c0, n) in res:
                p.add("act", lambda e, b=b, c0=c0, n=n: e.activation(ut[:, c0 - flo:c0 - flo + n], ps[b][:, 0:n], AF.Copy), reads=[("ps", b)], writes=[(tag, "ut")])
            tiles = [(t0, min(512, ncols - t0)) for t0 in range(0, ncols, 512)]
            for (t0, n) in tiles:
                bank = 3 + t0 // 512
                first = True
                cpos = t0
                while cpos < t0 + n:
                    cabs = flo + cpos
                    is_s = cabs >= NPB * 128
                    w = NS if is_s else 128
                    bi = fblocks.index(10 if is_s else cabs // 128)
                    lastc = (cpos + w >= t0 + n)
                    if is_s:
                        mm(ps[bank][:, cpos - t0:cpos - t0 + w], vbnS[0:NS, bi, j16 * 128:(j16 + 1) * 128], WsT[0:NS, g8, :], True, False,
                           reads=[(tag, "vbn", bi)] + [(tag, "WsT", x) for x in range(4)], writes=[("ps", bank)])
                        mm(ps[bank][:, cpos - t0:cpos - t0 + w], ones1[0:1, :], bss_hi[0:1, g8, :], False, False, reads=["ones1", (tag, "bssf", "hi")], writes=[("ps", bank)])
                        mm(ps[bank][:, cpos - t0:cpos - t0 + w], ones1[0:1, :], bss_lo[0:1, g8, :], False, True, reads=["ones1", (tag, "bssf", "lo")], writes=[("ps", bank)])
                    else:
                        mm(ps[bank][:, cpos - t0:cpos - t0 + w], vbnS[:, bi, j16 * 128:(j16 + 1) * 128], WmT[:, g8, :], True, False,
                           reads=[(tag, "vbn", bi), (tag, "WmT")], writes=[("ps", bank)])
                        mm(ps[bank][:, cpos - t0:cpos - t0 + w], ones1[0:1, :], bsr_hi[0:1, g8, :], False, False, reads=["ones1", (tag, "bsrf", "hi")], writes=[("ps", bank)])
                        mm(ps[bank][:, cpos - t0:cpos - t0 + w], ones1[0:1, :], bsr_lo[0:1, g8, :], False, True, reads=["ones1", (tag, "bsrf", "lo")], writes=[("ps", bank)])
                    cpos += w
                p.add("dve", lambda e, bank=bank, t0=t0, n=n: e.tensor_tensor(out=ut[:, t0:t0 + n], in0=ps[bank][:, 0:n], in1=ut[:, t0:t0 + n], op=ALU.mult),
                      reads=[("ps", bank), (tag, "ut")], writes=[(tag, "ut")])
                p.add("dve", lambda e, t0=t0, n=n, j16=j16: e.tensor_tensor(out=BT[:, j16, flo - 128 + t0:flo - 128 + t0 + n], in0=ut[:, t0:t0 + n], in1=szb[:, t0:t0 + n], op=ALU.mult),
                      reads=[(tag, "ut"), (tag, "szb")], writes=[("BT", j16)])
        if DEBUG and l == 0:
            p.dma("sp", dbg["B"], BT[:].rearrange("p a c -> p (a c)"), reads=[("BT", j) for j in range(16)], chan=U("dbgB"))
        return BT

    def phase_merge(l, BT):
        tag = U("mg")
        flo = 128 * (l + 1)
        ncols = NT - flo
        AT = sb_at("AT_%d" % l, [128, 16, NM], BF16, ZH)
        z3 = Zone(ZH + 40960 + 37376)
        sg = z3.sb(U("sg"), [128, NM], F32)
        tA = z3.sb(U("tA"), [128, NM], F32)
        mt = [z3.sb(U("mt"), [128, NM], BF16) for _ in range(2)]
        ga = p.dma_group(U("ald"))
        for c in range(16):
            p.dma("sp", AT[:, c, flo - 128:NM], spA[c, :, flo - 128:NM], reads=[("spA", c)], writes=[("AT", c)], group=ga)

        def act_tiles(buf, lo):
            out = []
            for (c0, n) in col_tiles(flo, NT):
                out.append((c0, n))
            return out

        for j in range(32):
            b2 = j % 2
            km = (tag, "mt", b2)
            if l == 0 and j < 24:
                mod_group(1, j, 6 + j % 2)
            res, _ = proj_fm(l, 84 + j, flo, NT, [0, 1, 2])
            for (b, c0, n) in res:
                p.add("act", lambda e, b=b, c0=c0, n=n: e.activation(sg[:, c0 - flo:c0 - flo + n], ps[b][:, 0:n], AF.Sigmoid), reads=[("ps", b)], writes=[(tag, "sg")])
            s = ring_load(w_pab[l, j])
            for ti, (c0, n) in enumerate(col_tiles(flo, NT)):
                b = 3 + ti
                for k in range(16):
                    mm(ps[b][:, 0:n], wr[:, s, k, :], AT[:, k, c0 - 128:c0 - 128 + n], k == 0, k == 15,
                       reads=[("ring", s)] + [("AT", k)], writes=[("ps", b)])
                p.add("dve", lambda e, b=b, c0=c0, n=n: e.tensor_tensor(out=tA[:, c0 - flo:c0 - flo + n], in0=ps[b][:, 0:n], in1=sg[:, c0 - flo:c0 - flo + n], op=ALU.mult),
                      reads=[("ps", b), (tag, "sg")], writes=[(tag, "tA")])
            res, _ = proj_fm(l, 116 + j, flo, NT, [0, 1, 2])
            for (b, c0, n) in res:
                p.add("act", lambda e, b=b, c0=c0, n=n: e.activation(sg[:, c0 - flo:c0 - flo + n], ps[b][:, 0:n], AF.Sigmoid), reads=[("ps", b)], writes=[(tag, "sg")])
            for ti, (c0, n) in enumerate(col_tiles(flo, NT)):
                b = 3 + ti
                for k in range(16):
                    mm(ps[b][:, 0:n], wr[:, s, 16 + k, :], BT[:, k, c0 - 128:c0 - 128 + n], k == 0, k == 15,
                       reads=[("ring", s)] + [("BT", k)], writes=[("ps", b)])
                p.add("dve", lambda e, b=b, c0=c0, n=n: e.tensor_tensor(out=sg[:, c0 - flo:c0 - flo + n], in0=ps[b][:, 0:n], in1=sg[:, c0 - flo:c0 - flo + n], op=ALU.mult),
                      reads=[("ps", b), (tag, "sg")], writes=[(tag, "sg")])
                p.add("dve", lambda e, c0=c0, n=n, b2=b2: e.tensor_tensor(out=mt[b2][:, c0 - flo:c0 - flo + n], in0=sg[:, c0 - flo:c0 - flo + n], in1=tA[:, c0 - flo:c0 - flo + n], op=ALU.add),
                      reads=[(tag, "sg"), (tag, "tA")], writes=[km])
            p.dma("sp", spM[j, :, flo - 128:NM], mt[b2][:, 0:ncols], reads=[km], writes=[("spM", j)], chan=f"{tag}m{b2}")
            if DEBUG and l == 0:
                p.dma("sp", dbg["M"][j, :, :], mt[b2][:, 0:ncols], reads=[km], chan=U("dbgM"))

    def phase_out(l, xsrc):
        tag = U("po")
        flo = 128 * (l + 1)
        fblocks = list(range(l + 1, NPB)) + [10]
        nfb = len(fblocks)
        npb_f = nfb - 1
        mT = hTm
        gm = p.dma_group(U("mld"))
        for j in range(32):
            p.dma("sp", mT[:, j, flo - 128:NM], spM[j, :, flo - 128:NM], reads=[("spM", j)], writes=[("mT", j)], group=gm)
        zz = Zone(ZH)
        gt = zz.sb(U("gt"), [128, D], F32)
        gts = zz.sb(U("gts"), [NS, D], F32)
        CW = 256
        xt = [zz.sb(U("xt"), [128, 10, CW], F32) for _ in range(2)]
        xs = [zz.sb(U("xs"), [NS, CW], F32) for _ in range(2)]
        rt = [zz.sb(U("rt"), [128, 10, CW], F32) for _ in range(2)]
        rs = [zz.sb(U("rs"), [NS, CW], F32) for _ in range(2)]
        gg = p.dma_group(U("gld"))
        p.dma("sp", gt[:], modd[l, 16:17, 2 * D:3 * D].broadcast_to([128, D]), reads=[("modd", l)], writes=[(tag, "gt")], group=gg)
        p.dma("sp", gts[:], modd[l, 0:NS, 2 * D:3 * D], reads=[("modd", l)], writes=[(tag, "gts")], group=gg)
        r0 = flo
        for g in range(D // CW):
            b2 = g % 2
            c0 = g * CW
            kx, kr = (tag, "xt", b2), (tag, "rt", b2)
            kxs, krs = (tag, "xs", b2), (tag, "rs", b2)
            gx = p.dma_group(f"{tag}x{b2}")
            p.dma("sp", xt[b2][:, 0:npb_f, :], xsrc[r0:r0 + npb_f * 128, c0:c0 + CW].rearrange("(t p) c -> p t c", p=128),
                  reads=[("xsrc",)], writes=[kx], group=gx)
            p.dma("sp", xs[b2][:], xsrc[NPB * 128:NT, c0:c0 + CW], reads=[("xsrc",)], writes=[kxs], group=gx)
            s = ring_pair_load(w_o[l, 2 * g:2 * g + 2])
            for i, blk in enumerate(fblocks):
                npart = NS if blk == 10 else 128
                bank = i // 2
                q = i % 2
                if blk == 10:
                    lhs = lambda j: mT[:, j, 1152:1168]
                else:
                    lhs = lambda j, blk=blk: mT[:, j, (blk - 1) * 128:blk * 128]
                for j in range(NJ):
                    mm(ps[bank][0:npart, q * CW:(q + 1) * CW].rearrange("p (a c) -> p a c", c=128), lhs(j), wr[:, s:s + 2, j, :], j == 0, j == NJ - 1,
                       reads=[("ring", s), ("ring", s + 1), ("mT", j)], writes=[("ps", bank)])
            for i, blk in enumerate(fblocks):
                npart = NS if blk == 10 else 128
                bank = i // 2
                q = i % 2
                if blk == 10:
                    p.add("dve", lambda e, bank=bank, q=q, b2=b2, c0=c0: e.tensor_tensor(out=rs[b2][:, :], in0=ps[bank][0:NS, q * CW:(q + 1) * CW], in1=gts[:, c0:c0 + CW], op=ALU.mult),
                          reads=[("ps", bank), (tag, "gts")], writes=[krs])
                    p.add("dve", lambda e, b2=b2: e.scalar_tensor_tensor(out=rs[b2][:, :], in0=xs[b2][:, :], scalar=ALPHA, in1=rs[b2][:, :], op0=ALU.mult, op1=ALU.add),
                          reads=[kxs, krs], writes=[krs])
                else:
                    p.add("dve", lambda e, bank=bank, q=q, b2=b2, c0=c0, i=i: e.tensor_tensor(out=rt[b2][:, i, :], in0=ps[bank][:, q * CW:(q + 1) * CW], in1=gt[:, c0:c0 + CW], op=ALU.mult),
                          reads=[("ps", bank), (tag, "gt")], writes=[kr])
                    p.add("dve", lambda e, b2=b2, i=i: e.scalar_tensor_tensor(out=rt[b2][:, i, :], in0=xt[b2][:, i, :], scalar=ALPHA, in1=rt[b2][:, i, :], op0=ALU.mult, op1=ALU.add),
                          reads=[kx, kr], writes=[kr])
            go = p.dma_group(f"{tag}o{b2}")
            p.dma("sp", spR[r0:r0 + npb_f * 128, c0:c0 + CW].rearrange("(t p) c -> p t c", p=128), rt[b2][:, 0:npb_f, :],
                  reads=[kr], writes=[("spR",)], group=go)
            p.dma("sp", spR[NPB * 128:NT, c0:c0 + CW], rs[b2][:], reads=[krs], writes=[("spRs",)], group=go)
        if DEBUG and l == 0:
            pass

    p.dma("sp", tmpf[:, 512:768], smask_in.rearrange("p a c -> p (a c)"), writes=["c_t2"], chan="c2")
    p.add("dve", lambda e: e.tensor_copy(smask[:].rearrange("p a c -> p (a c)"), tmpf[:, 512:768]), reads=["c_t2"], writes=["smask"])
    barrier()
    mod_setup()
    barrier()
    for grp in range(16):
        mod_group(0, grp, grp % 2)
    barrier()
    phase_h(0, x_in, list(range(0, NPB)) + [10], False, None)
    if DEBUG:
        p.dma("sp", dbg["hT"], hTm[:].rearrange("p a c -> p (a c)"), reads=[("hT", b) for b in range(11)], chan=U("dbgh"))
    barrier()
    for l in range(DEPTH):
        phase_attn(l)
        barrier()
        BT = phase_sgu(l)
        barrier()
        phase_merge(l, BT)
        barrier()
        phase_out(l, x_in if l == 0 else spX)
        barrier()
        if l == 0:
            def xdst(blk):
                if blk == 10:
                    return spX[NPB * 128:NT, :]
                return spX[blk * 128:(blk + 1) * 128, :]
            phase_h(1, spR, list(range(1, NPB)) + [10], True, xdst)
        else:
            def ydst(blk):
                if blk == 10:
                    return y_s
                return y_p[(blk - 2) * 128:(blk - 1) * 128, :]
            phase_h(2, spR, list(range(2, NPB)) + [10], True, ydst)
        barrier()
    p.emit(nc)
    return nc


def _blocked(W, cw=128):
    K, N = W.shape
    return np.ascontiguousarray(W.reshape(K // 128, 128, N // cw, cw).transpose(2, 1, 0, 3).reshape(N // cw, 128, (K // 128) * cw))


_NC_CACHE = {}


def kernel(x_prompt, x_sample, cache_k, cache_v, c_prompt, c_sample, w_ada, b_ada, w_in, attn_sinks,
           sgu_ln_gain, sgu_ln_bias, sgu_w_s, sgu_b_s, w_pa, w_pb, w_o, ln_gain, ln_bias):
    f32 = np.float32
    A = lambda a: np.asarray(a, dtype=f32)
    x_prompt, x_sample, cache_k, cache_v = A(x_prompt), A(x_sample), A(cache_k), A(cache_v)
    c_prompt, c_sample, w_ada, b_ada, w_in = A(c_prompt), A(c_sample), A(w_ada), A(b_ada), A(w_in)
    attn_sinks, sgu_ln_gain, sgu_ln_bias, sgu_w_s, sgu_b_s = A(attn_sinks), A(sgu_ln_gain), A(sgu_ln_bias), A(sgu_w_s), A(sgu_b_s)
    w_pa, w_pb, w_o, ln_gain, ln_bias = A(w_pa), A(w_pb), A(w_o), A(ln_gain), A(ln_bias)

    heads = []
    for c in range(16):
        m, i = c // 8, c % 8
        heads += [(2 * m) * 8 + i, (2 * m + 1) * 8 + i]
    hperm = np.concatenate([np.arange(h * 64, h * 64 + 64) for h in heads])
    q0, k0, v0, za0, u0, vb0, zb0, ga0, gb0 = 0, 2048, 2304, 2560, 4608, 6656, 8704, 10752, 14848
    cols = np.concatenate([np.arange(k0, k0 + 256), np.arange(v0, v0 + 256), q0 + hperm, za0 + hperm,
                           np.arange(vb0, vb0 + 2048), np.arange(u0, u0 + 2048), np.arange(zb0, zb0 + 2048),
                           np.arange(ga0, ga0 + 4096), np.arange(gb0, gb0 + 4096)])
    w_in_b = np.stack([_blocked(w_in[l][:, cols]) for l in range(DEPTH)])
    w_ada_b = np.stack([_blocked(w_ada[l]) for l in range(DEPTH)])
    w_pab_b = np.stack([_blocked(np.concatenate([w_pa[l][hperm, :], w_pb[l]], axis=0)) for l in range(DEPTH)])
    w_o_b = np.stack([_blocked(w_o[l]) for l in range(DEPTH)])
    sinks_pp = np.zeros((DEPTH, 128, 16), f32)
    for l in range(DEPTH):
        for c in range(16):
            sinks_pp[l, 0:64, c] = attn_sinks[l, heads[2 * c]]
            sinks_pp[l, 64:128, c] = attn_sinks[l, heads[2 * c + 1]]
    identf = np.eye(128, dtype=f32)
    li = np.arange(128)[:, None]
    qi = np.arange(128)[None, :]
    msame = np.where(li <= qi, 0.0, NEG).astype(f32)
    mprev = np.where(li > qi, 0.0, NEG).astype(f32)
    maskA = np.concatenate([msame, mprev], axis=1)
    maskB = np.concatenate([msame, np.full((128, 128), NEG, f32)], axis=1)
    tril = np.tril(np.ones((128, 128), f32))
    smask = np.full((128, 2, 128), NEG, f32)
    smask[:, 0, 0:4] = np.where(li > np.arange(4)[None, :], 0.0, NEG)
    for a in range(16):
        for b in range(16):
            if a // 4 == b // 4 and a % 4 <= b % 4:
                smask[a, 1, b] = 0.0

    in_maps = []
    for cid in range(NCORES):
        b, half = cid // 2, cid % 2
        start = half * 1024
        x_core = np.zeros((NT, D), f32)
        lo = start - 256
        src_lo = max(lo, 0)
        x_core[src_lo - lo:NPB * 128] = x_prompt[b, src_lo:start + 1024]
        x_core[NPB * 128:] = x_sample[4 * cid:4 * cid + 4].reshape(NS, D)
        crow = np.zeros((32, D), f32)
        for r in range(NS):
            crow[r] = c_sample[4 * cid + r // 4]
        crow[16] = c_prompt[b]
        cT = np.ascontiguousarray(crow.reshape(32, NJ, 128).transpose(2, 1, 0).reshape(128, NJ * 32))
        ck = np.ascontiguousarray(cache_k[:, 4 * cid:4 * cid + 4].reshape(DEPTH, 4, 128, 256))
        cv = np.ascontiguousarray(cache_v[:, 4 * cid:4 * cid + 4].reshape(DEPTH, 4, 128, 256))
        in_maps.append({
            "x_in": x_core, "cT": cT, "ck": ck, "cv": cv, "w_ada": w_ada_b, "b_ada": b_ada, "w_in": w_in_b,
            "w_pab": w_pab_b, "w_o": w_o_b, "sinks": sinks_pp, "sgu_g": sgu_ln_gain, "sgu_b": sgu_ln_bias,
            "w_s": sgu_w_s, "b_s": sgu_b_s, "ln_g": ln_gain, "ln_b": ln_bias, "identf": identf,
            "maskA": maskA, "mask01": (maskA if half == 1 else maskB), "tril": tril, "smask": smask,
        })
    if "nc" not in _NC_CACHE:
        _NC_CACHE["nc"] = build_program()
    nc = _NC_CACHE["nc"]
    res = run_bass_kernel_spmd(nc, in_maps, core_ids=list(range(NCORES)))
    R = res.results
    kernel.last_results = R
    y_prompt = np.zeros((4, 2048, D), f32)
    y_sample = np.zeros((32, 4, D), f32)
    win_k = np.zeros((DEPTH, 4, 128, 4, 64), f32)
    win_v = np.zeros((DEPTH, 4, 128, 4, 64), f32)
    new_k = np.zeros((DEPTH, 32, 4, 4, 64), f32)
    new_v = np.zeros((DEPTH, 32, 4, 4, 64), f32)
    sgu_v = np.zeros((DEPTH, 32, 4, 2048), f32)
    for cid in range(NCORES):
        b, half = cid // 2, cid % 2
        r = R[cid]
        y_prompt[b, half * 1024:(half + 1) * 1024] = np.asarray(r["y_p"])
        y_sample[4 * cid:4 * cid + 4] = np.asarray(r["y_s"]).reshape(4, 4, D)
        if half == 1:
            win_k[:, b] = np.asarray(r["o_wk"]).reshape(DEPTH, 128, 4, 64)
            win_v[:, b] = np.asarray(r["o_wv"]).reshape(DEPTH, 128, 4, 64)
        new_k[:, 4 * cid:4 * cid + 4] = np.asarray(r["o_nk"]).reshape(DEPTH, 4, 4, 4, 64)
        new_v[:, 4 * cid:4 * cid + 4] = np.asarray(r["o_nv"]).reshape(DEPTH, 4, 4, 4, 64)
        sgu_v[:, 4 * cid:4 * cid + 4] = np.asarray(r["o_sv"]).reshape(DEPTH, 4, 4, 2048)
    return (y_prompt, y_sample, win_k, win_v, new_k, new_v, sgu_v)
```

```python
import contextlib
import numpy as np
import concourse.bass as bass
import concourse.mybir as mybir
from concourse.bass_utils import run_bass_kernel_spmd

F32 = mybir.dt.float32
BF16 = mybir.dt.bfloat16
AF = mybir.ActivationFunctionType
ALU = mybir.AluOpType

D = 4096
NJ = 32
DEPTH = 2
NCORES = 8
NPB = 10
NS = 16
NT = NPB * 128 + NS
NM = NT - 128
ALPHA = (2 * DEPTH) ** 0.25
LN_EPS = 1e-5
NEG = -30000.0
DEBUG = False


class Op:
    __slots__ = ("eng", "fn", "deps", "is_dma", "group", "needs_inc", "tok", "idx")


class DmaGroup:
    __slots__ = ("chan", "ops", "val", "prev")

    def __init__(self, chan):
        self.chan = chan
        self.ops = []
        self.val = None
        self.prev = None


class Prog:
    ENGS = ("pe", "act", "dve", "pool", "sp")

    def __init__(self, sem_rot=4):
        self.ops = []
        self.state = {}
        self.chan_groups = {}
        self.sem_rot = sem_rot
        self.pending = []
        self.auto_n = 0

    def add(self, eng, fn, reads=(), writes=(), group=None, extra_deps=()):
        op = Op()
        op.eng = eng
        op.fn = fn
        op.is_dma = group is not None
        op.group = group
        op.needs_inc = op.is_dma
        op.tok = None
        op.idx = len(self.ops)
        deps = set()
        st = self.state
        psr = [k for k in reads if isinstance(k, tuple) and k and k[0] == "ps"]
        if psr:
            reads = [k for k in reads if k not in psr]
            writes = list(writes) + [k for k in psr if k not in writes]
        for k in reads:
            s = st.get(k)
            if s is not None and s[0] is not None:
                deps.add(s[0])
        for k in writes:
            s = st.get(k)
            if s is not None:
                if s[0] is not None:
                    deps.add(s[0])
                for r in s[1]:
                    deps.add(r)
        deps.update(extra_deps)
        fdeps = []
        for d in deps:
            if d is op:
                continue
            if (not d.is_dma) and d.eng == "pe" and eng == "pe" and not op.is_dma:
                continue
            fdeps.append(d)
            d.needs_inc = True
        op.deps = fdeps
        if group is not None:
            for d in fdeps:
                assert d.group is not group, "intra-group DMA dependency"
        for k in reads:
            s = st.get(k)
            if s is None:
                st[k] = [None, [op]]
            else:
                if not op.is_dma:
                    s[1] = [r for r in s[1] if r.is_dma or r.eng != eng]
                s[1].append(op)
        for k in writes:
            st[k] = [op, []]
        if group is not None:
            group.ops.append(op)
            if eng == "sp":
                self.pending.append(op)
        self.ops.append(op)
        return op

    NPOOL = 24

    def dma_group(self, chan):
        if chan is None or not str(chan).startswith("ring"):
            chan = "auto%d" % (self.auto_n % self.NPOOL)
            self.auto_n += 1
        g = DmaGroup(chan)
        lst = self.chan_groups.setdefault(chan, [])
        g.prev = lst[-1] if lst else None
        lst.append(g)
        return g

    def dma(self, queue, out, in_, reads=(), writes=(), chan=None, group=None, **kw):
        if group is None:
            group = self.dma_group(chan)
        ed = tuple(group.prev.ops) if group.prev is not None else ()
        return self.add(queue, lambda e: e.dma_start(out=out, in_=in_, **kw), reads, writes, group=group, extra_deps=ed)

    def emit(self, nc):
        R = self.sem_rot
        with contextlib.ExitStack() as es:
            esem = {e: [es.enter_context(nc.semaphore(f"p_{e}{i}")) for i in range(R)] for e in self.ENGS}
            csem = {}
            for chan, groups in self.chan_groups.items():
                csem[chan] = es.enter_context(nc.semaphore(f"d_{chan}"))
                cum = 0
                for g in groups:
                    cum += 16 * len(g.ops)
                    g.val = cum
                    for o in g.ops:
                        o.tok = (("c", chan), cum)
            cnt = {e: 0 for e in self.ENGS}
            for o in self.ops:
                if o.is_dma:
                    continue
                if o.needs_inc:
                    m = cnt[o.eng]
                    cnt[o.eng] += 1
                    o.tok = (("e", o.eng, m % R), m // R + 1)
            per_eng = {e: [o for o in self.ops if o.eng == e] for e in self.ENGS}
            block = es.enter_context(nc.Block())

            def run(e_name, e):
                known = {}
                for o in per_eng[e_name]:
                    need_e = {}
                    need_c = {}
                    for d in o.deps:
                        k, v = d.tok
                        if k[0] == "e":
                            m = (v - 1) * R + k[2]
                            if need_e.get(k[1], -1) < m:
                                need_e[k[1]] = m
                        else:
                            if need_c.get(k, 0) < v:
                                need_c[k] = v
                    for en, m in need_e.items():
                        if known.get(("E", en), -1) >= m:
                            continue
                        known[("E", en)] = m
                        e.wait_ge(esem[en][m % R], m // R + 1)
                    for k, v in need_c.items():
                        if known.get(k, 0) >= v:
                            continue
                        known[k] = v
                        e.wait_ge(csem[k[1]], v)
                    ins = o.fn(e)
                    if o.needs_inc:
                        if o.is_dma:
                            ins.then_inc(csem[o.tok[0][1]], 16)
                        else:
                            ins.then_inc(esem[o.eng][o.tok[0][2]], 1)
                if e_name == "sp":
                    for chan, groups in self.chan_groups.items():
                        v = groups[-1].val
                        if known.get(("c", chan), 0) < v:
                            e.wait_ge(csem[chan], v)

            @block.tensor
            def _(e):
                run("pe", e)

            @block.scalar
            def _(e):
                run("act", e)

            @block.vector
            def _(e):
                run("dve", e)

            @block.gpsimd
            def _(e):
                run("pool", e)

            @block.sync
            def _(e):
                run("sp", e)


def col_tiles(lo, hi):
    out = []
    c = lo
    while c < hi:
        n = hi - c
        if n > 512:
            n = 384
        out.append((c, n))
        c += n
    return out


def blk_keys(prefix, lo, n):
    return [(prefix, b) for b in range(lo // 128, (lo + n - 1) // 128 + 1)]


def build_program():
    nc = bass.Bass("TRN2", target_bir_lowering=False)
    p = Prog()

    def din(name, shape, dt=F32):
        return nc.dram_tensor(name, list(shape), dt, kind="ExternalInput").ap()

    def dout(name, shape, dt=F32):
        return nc.dram_tensor(name, list(shape), dt, kind="ExternalOutput").ap()

    def dint(name, shape, dt=F32):
        return nc.dram_tensor(name, list(shape), dt, kind="Internal").ap()

    x_in = din("x_in", [NT, D])
    cT_in = din("cT", [128, NJ * 32])
    ck_in = din("ck", [DEPTH, 4, 128, 256])
    cv_in = din("cv", [DEPTH, 4, 128, 256])
    w_ada = din("w_ada", [DEPTH, 96, 128, 4096])
    b_ada = din("b_ada", [DEPTH, 12288])
    w_in = din("w_in", [DEPTH, 148, 128, 4096])
    w_pab = din("w_pab", [DEPTH, 32, 128, 4096])
    w_o = din("w_o", [DEPTH, 32, 128, 4096])
    sinks_in = din("sinks", [DEPTH, 128, 16])
    sgu_g = din("sgu_g", [DEPTH, 2048])
    sgu_b = din("sgu_b", [DEPTH, 2048])
    w_s = din("w_s", [DEPTH, 8, 128, 128])
    b_s = din("b_s", [DEPTH, 8, 128])
    ln_g = din("ln_g", [DEPTH, D])
    ln_b = din("ln_b", [DEPTH, D])
    identf_in = din("identf", [128, 128])
    maskA_in = din("maskA", [128, 256])
    mask01_in = din("mask01", [128, 256])
    tril_in = din("tril", [128, 128])
    smask_in = din("smask", [128, 2, 128])

    y_p = dout("y_p", [1024, D])
    y_s = dout("y_s", [NS, D])
    o_wk = dout("o_wk", [DEPTH, 128, 256])
    o_wv = dout("o_wv", [DEPTH, 128, 256])
    o_nk = dout("o_nk", [DEPTH, NS, 256])
    o_nv = dout("o_nv", [DEPTH, NS, 256])
    o_sv = dout("o_sv", [DEPTH, NS, 2048])

    modd = dint("modd", [DEPTH, 32, 12288])
    spA = dint("spA", [16, 128, NM], BF16)
    spM = dint("spM", [32, 128, NM], BF16)
    spR = dint("spR", [NT, D])
    spX = dint("spX", [NT, D])

    dbg = {}
    if DEBUG:
        dbg["hT"] = dout("dbg_hT", [128, NJ * NM], BF16)
        dbg["A"] = dout("dbg_A", [16, 128, NM], BF16)
        dbg["B"] = dout("dbg_B", [128, 16 * NM], BF16)
        dbg["M"] = dout("dbg_M", [32, 128, NM], BF16)
        dbg["R"] = dout("dbg_R", [NT, D])

    off = [16512]

    def sb_at(name, shape, dt, offset):
        return nc.alloc_sbuf_tensor_at(name, list(shape), dt, offset=offset)

    def sb(name, shape, dt):
        nb = int(np.prod(shape[1:])) * (4 if dt == F32 else 2)
        o = off[0]
        off[0] = (o + nb + 63) // 64 * 64
        return sb_at(name, shape, dt, o)

    wr = sb("wr", [128, 4, NJ, 128], BF16)
    identf = sb("identf", [128, 128], F32)
    identb = sb("identb", [128, 128], BF16)
    maskA = sb("maskA", [128, 256], BF16)
    mask01 = sb("mask01", [128, 256], BF16)
    trilf = sb("trilf", [128, 128], F32)
    onesp = sb("onesp", [128, 2, 128], BF16)
    ones1 = sb("ones1", [1, 128], BF16)
    esink = sb("esink", [128, 16], F32)
    stats = sb("stats", [128, 11, 2], F32)
    smallf = sb("smallf", [128, 64], F32)
    scT = sb("scT", [128, NJ, 32], BF16)
    mod_bt = sb("mod_bt", [32, 512], F32)
    mod_mt = sb("mod_mt", [32, 512], F32)
    smask = sb("smask", [128, 2, 128], BF16)
    s_po = sb("s_po", [NS, 32], BF16)
    s_pc = sb("s_pc", [128, 32], BF16)
    hTm = sb("hTm", [128, NJ, NM], BF16)
    ZH = off[0]
    hTh = sb("hTh", [128, NJ, 128], BF16)
    Z0 = off[0]
    ZEND = 229376 - 128
    assert ZEND - Z0 > 86000, (Z0,)
    PZ = ZEND - 7168
    WmT = sb_at("WmT", [128, 8, 128], BF16, PZ)
    WsT = sb_at("WsT", [16, 8, 16], BF16, PZ + 2048)
    bsr_hi = sb_at("bsr_hi", [1, 8, 128], BF16, PZ + 2304)
    bsr_lo = sb_at("bsr_lo", [1, 8, 128], BF16, PZ + 4352)
    bss_hi = sb_at("bss_hi", [1, 8, 16], BF16, PZ + 6400)
    bss_lo = sb_at("bss_lo", [1, 8, 16], BF16, PZ + 6656)

    class Zone:
        def __init__(self, base):
            self.o = base

        def sb(self, name, shape, dt):
            nb = int(np.prod(shape[1:])) * (4 if dt == F32 else 2)
            o = self.o
            self.o = (o + nb + 63) // 64 * 64
            assert self.o <= ZEND, (name, self.o)
            return sb_at(name, shape, dt, o)

    ps = [nc.alloc_psum_tensor(f"ps{i}", [128, 512], F32) for i in range(8)]

    uid = [0]

    def U(s):
        uid[0] += 1
        return f"{s}_{uid[0]}"

    ring_n = [0]

    def ring_load(src):
        s = ring_n[0] % 4
        ring_n[0] += 1
        p.dma("pool", wr[:, s, :, :], src.rearrange("p (j c) -> p j c", c=128), writes=[("ring", s)], chan=f"ring{s}")
        return s

    def ring_pair_load(src2):
        if ring_n[0] % 2:
            ring_n[0] += 1
        s = ring_n[0] % 4
        ring_n[0] += 2
        p.dma("pool", wr[:, s:s + 2, :, :], src2.rearrange("n p (j c) -> p n j c", c=128),
              writes=[("ring", s), ("ring", s + 1)], chan=f"ring{s}")
        return s

    def mm(out, lhsT, rhs, start, stop, reads, writes):
        p.add("pe", lambda e: e.matmul(out, lhsT=lhsT, rhs=rhs, start=start, stop=stop), reads, writes)

    barrier_n = [0]
    bar_t = sb_at("bar_t", [128, 8], F32, ZEND)
    bar_b = sb_at("bar_b", [128, 8], BF16, ZEND + 64)

    def barrier():
        n = barrier_n[0]
        barrier_n[0] += 1
        p.add("pe", lambda e: e.matmul(ps[7][0:8, 0:8], lhsT=bar_b[:, 0:8], rhs=bar_b[:, 0:8], start=True, stop=True),
              reads=[("barb",)], writes=[("barA", n, "pe"), ("ps", 7)])
        p.add("act", lambda e: e.activation(bar_t[:, 0:1], bar_t[:, 4:5], AF.Copy), reads=[("bart",)], writes=[("barA", n, "act"), ("bt", "act")])
        p.add("dve", lambda e: e.tensor_copy(bar_t[:, 1:2], bar_t[:, 5:6]), reads=[("bart",)], writes=[("barA", n, "dve"), ("bt", "dve")])
        pend = list(p.pending)
        p.pending = []
        p.add("sp", lambda e: e.nop(), writes=[("barA", n, "sp")], extra_deps=pend)
        allk = [("barA", n, x) for x in ("pe", "act", "dve", "sp")]
        p.add("pe", lambda e: e.matmul(ps[7][0:8, 0:8], lhsT=bar_b[:, 0:8], rhs=bar_b[:, 0:8], start=True, stop=True),
              reads=allk + [("barb",)], writes=[("ps", 7)])
        p.add("act", lambda e: e.activation(bar_t[:, 0:1], bar_t[:, 4:5], AF.Copy), reads=allk + [("bart",)], writes=[("bt", "act")])
        p.add("dve", lambda e: e.tensor_copy(bar_t[:, 1:2], bar_t[:, 5:6]), reads=allk + [("bart",)], writes=[("bt", "dve")])
        p.add("sp", lambda e: e.nop(), reads=allk)

    z = Zone(Z0)
    tmpf = z.sb("c_tmpf", [128, 1024], F32)
    p.add("dve", lambda e: e.memset(bar_t[:], 0.0), writes=[("bart",)])
    p.add("dve", lambda e: e.memset(bar_b[:], 0.0), writes=[("barb",)])
    g0 = p.dma_group("c0")
    p.dma("sp", identf[:], identf_in, writes=["identf"], group=g0)
    p.dma("sp", trilf[:], tril_in, writes=["trilf"], group=g0)
    p.dma("sp", tmpf[:, 0:256], maskA_in, writes=["c_t0"], group=g0)
    p.dma("sp", tmpf[:, 256:512], mask01_in, writes=["c_t1"], group=g0)
    p.add("dve", lambda e: e.tensor_copy(identb[:], identf[:]), reads=["identf"], writes=["identb"])
    p.add("dve", lambda e: e.tensor_copy(maskA[:], tmpf[:, 0:256]), reads=["c_t0"], writes=["maskA"])
    p.add("dve", lambda e: e.tensor_copy(mask01[:], tmpf[:, 256:512]), reads=["c_t1"], writes=["mask01"])
    p.add("dve", lambda e: e.memset(onesp[:], 0.0), writes=["onesp"])
    p.add("dve", lambda e: e.memset(onesp[:, 0, 0:64], 1.0), writes=["onesp"])
    p.add("dve", lambda e: e.memset(onesp[:, 1, 64:128], 1.0), writes=["onesp"])
    p.add("dve", lambda e: e.memset(ones1[:], 1.0), writes=["ones1"])

    def mod_setup():
        zc = Zone(Z0)
        cTf = zc.sb("c_cTf", [128, NJ * 32], F32)
        p.dma("sp", cTf[:], cT_in, writes=["cTf"], chan="c1")
        p.add("act", lambda e: e.activation(scT[:].rearrange("p j m -> p (j m)"), cTf[:], AF.Silu), reads=["cTf"], writes=["scT"])

    def mod_group(l, grp, bank, halves=(0, 1)):
        pk = ("ps", bank)
        if 0 in halves:
            p.dma("sp", mod_bt[:], b_ada[l:l + 1, grp * 512:(grp + 1) * 512].broadcast_to([32, 512]), writes=[("mod_bt",)], chan="mbt")
        for q in halves:
            nb = grp * 4 + q * 2
            s = ring_pair_load(w_ada[l, nb:nb + 2])
            for j in range(NJ):
                mm(ps[bank][0:32, q * 256:(q + 1) * 256].rearrange("p (a c) -> p a c", c=128), scT[:, j, :], wr[:, s:s + 2, j, :], j == 0, j == NJ - 1,
                   reads=[("ring", s), ("ring", s + 1), "scT"], writes=[pk])
        if 1 not in halves:
            return
        p.add("dve", lambda e: e.tensor_tensor(out=mod_mt[:], in0=ps[bank][0:32, :], in1=mod_bt[:], op=ALU.add),
              reads=[pk, ("mod_bt",)], writes=[("mod_mt",)])
        p.dma("sp", modd[l, :, grp * 512:(grp + 1) * 512], mod_mt[:], reads=[("mod_mt",)], writes=[("modd", l)], chan="mmo")

    def hT_dst(blk, j0, nj):
        if blk == 0:
            return hTh[:, j0:j0 + nj, :]
        if blk == 10:
            return hTm[:, j0:j0 + nj, 1152:1168]
        return hTm[:, j0:j0 + nj, (blk - 1) * 128:blk * 128]

    def phase_h(l, src, blocks, do_ln, xdst):
        zz = Zone(Z0)
        H = 2048
        last = l >= DEPTH
        if not last:
            s1 = zz.sb(U("s1"), [128, H], F32)
            sh = zz.sb(U("sh"), [128, H], F32)
            s1s = zz.sb(U("s1s"), [NS, H], F32)
            shs = zz.sb(U("shs"), [NS, H], F32)
        if do_ln:
            gg = zz.sb(U("gg"), [128, H], F32)
            bb = zz.sb(U("bb"), [128, H], F32)
        xt = [zz.sb(U("xt"), [128, H], F32) for _ in range(2)]
        tt = [zz.sb(U("tt"), [128, H], F32) for _ in range(2)]
        hb_ = zz.sb(U("hb"), [128, H], BF16)
        hb = [hb_, hb_]
        tag = U("ph")
        if do_ln:
            rt = [xt[0], tt[0]]
            rtk = [(tag, "xt", 0), (tag, "tt", 0)]
            bst = zz.sb(U("bst"), [128, 8, 6], F32)
            mv = zz.sb(U("mv"), [128, 2], F32)
            for i, blk in enumerate(blocks):
                npart = NS if blk == 10 else 128
                r0 = blk * 128
                for hf in range(2):
                    p.dma("sp", rt[hf][0:npart, :], src[r0:r0 + npart, hf * H:(hf + 1) * H],
                          reads=[("spR",), ("spRs",)], writes=[rtk[hf]], chan=f"{tag}r{hf}")
                    for q in range(4):
                        p.add("dve", lambda e, hf=hf, q=q, npart=npart: e.bn_stats(bst[0:npart, hf * 4 + q, :], rt[hf][0:npart, q * 512:(q + 1) * 512]),
                              reads=[rtk[hf]], writes=[(tag, "bst")])
                p.add("dve", lambda e, npart=npart: e.bn_aggr(mv[0:npart, :], bst[0:npart, :, :]), reads=[(tag, "bst")], writes=[(tag, "mv")])
                p.add("dve", lambda e, npart=npart: e.tensor_scalar_add(mv[0:npart, 1:2], mv[0:npart, 1:2], LN_EPS), reads=[(tag, "mv")], writes=[(tag, "mv")])
                p.add("act", lambda e, npart=npart: e.activation(smallf[0:npart, 4:5], mv[0:npart, 1:2], AF.Sqrt), reads=[(tag, "mv")], writes=[(tag, "mvs")])
                p.add("dve", lambda e, blk=blk, npart=npart: e.reciprocal(stats[0:npart, blk, 0:1], smallf[0:npart, 4:5]),
                      reads=[(tag, "mvs")], writes=[("stats", blk)])
                p.add("dve", lambda e, blk=blk, npart=npart: e.scalar_tensor_tensor(out=stats[0:npart, blk, 1:2], in0=mv[0:npart, 0:1], scalar=-1.0, in1=stats[0:npart, blk, 0:1], op0=ALU.mult, op1=ALU.mult),
                      reads=[(tag, "mv"), ("stats", blk)], writes=[("stats", blk)])
        for hf in range(2):
            c0 = hf * H
            kh = (tag, "par", hf)
            g = p.dma_group(f"{tag}p{hf}")
            wk = [(tag, "s1"), (tag, "sh"), (tag, "s1s"), (tag, "shs"), (tag, "gg"), (tag, "bb")]
            if not last:
                p.dma("sp", s1[:], modd[l, 16:17, D + c0:D + c0 + H].broadcast_to([128, H]), reads=[("modd", l)], writes=[wk[0]], group=g)
                p.dma("sp", sh[:], modd[l, 16:17, c0:c0 + H].broadcast_to([128, H]), reads=[("modd", l)], writes=[wk[1]], group=g)
                p.dma("sp", s1s[:], modd[l, 0:NS, D + c0:D + c0 + H], reads=[("modd", l)], writes=[wk[2]], group=g)
                p.dma("sp", shs[:], modd[l, 0:NS, c0:c0 + H], reads=[("modd", l)], writes=[wk[3]], group=g)
            if do_ln:
                p.dma("sp", gg[:], ln_g[l - 1:l, c0:c0 + H].broadcast_to([128, H]), writes=[wk[4]], group=g)
                p.dma("sp", bb[:], ln_b[l - 1:l, c0:c0 + H].broadcast_to([128, H]), writes=[wk[5]], group=g)
            if not last:
                p.add("dve", lambda e: e.tensor_scalar_add(s1[:], s1[:], 1.0), reads=[wk[0]], writes=[wk[0]])
                p.add("dve", lambda e: e.tensor_scalar_add(s1s[:], s1s[:], 1.0), reads=[wk[2]], writes=[wk[2]])
            def issue_load(i_):
                blk_ = blocks[i_]
                np_ = NS if blk_ == 10 else 128
                p.dma("sp", xt[i_ % 2][0:np_, :], src[blk_ * 128:blk_ * 128 + np_, c0:c0 + H], reads=[("spR",), ("spRs",)],
                      writes=[(tag, "xt", i_ % 2)], chan=f"{tag}x{i_ % 2}")
            issue_load(0)
            for i, blk in enumerate(blocks):
                b2 = i % 2
                npart = NS if blk == 10 else 128
                r0 = blk * 128
                kx, kt, kb_ = (tag, "xt", b2), (tag, "tt", b2), (tag, "hb", 0)
                if i + 1 < len(blocks):
                    issue_load(i + 1)
                if do_ln:
                    p.add("act", lambda e, b2=b2, blk=blk, npart=npart: e.activation(xt[b2][0:npart, :], xt[b2][0:npart, :], AF.Identity,
                                                                             bias=stats[0:npart, blk, 1:2], scale=stats[0:npart, blk, 0:1]),
                          reads=[kx, ("stats", blk)], writes=[kx])
                    p.add("dve", lambda e, b2=b2, npart=npart: e.tensor_tensor(out=xt[b2][0:npart, :], in0=xt[b2][0:npart, :], in1=gg[0:npart, :], op=ALU.mult),
                          reads=[kx, wk[4]], writes=[kx])
                    p.add("dve", lambda e, b2=b2, npart=npart: e.tensor_tensor(out=xt[b2][0:npart, :], in0=xt[b2][0:npart, :], in1=bb[0:npart, :], op=ALU.add),
                          reads=[kx, wk[5]], writes=[kx])
                    dst = xdst(blk)
                    if dst is not None:
                        p.dma("sp", dst[:, c0:c0 + H], xt[b2][0:npart, :], reads=[kx], writes=[("xdst", blk)], chan=f"{tag}o{b2}")
                if last:
                    continue
                a1 = s1s if blk == 10 else s1
                a2 = shs if blk == 10 else sh
                k1 = wk[2] if blk == 10 else wk[0]
                k2 = wk[3] if blk == 10 else wk[1]
                p.add("dve", lambda e, b2=b2, npart=npart, a1=a1: e.tensor_tensor(out=tt[b2][0:npart, :], in0=xt[b2][0:npart, :], in1=a1[0:npart, :], op=ALU.mult),
                      reads=[kx, k1], writes=[kt])
                p.add("dve", lambda e, b2=b2, npart=npart, a2=a2: e.tensor_tensor(out=hb[b2][0:npart, :], in0=tt[b2][0:npart, :], in1=a2[0:npart, :], op=ALU.add),
                      reads=[kt, k2], writes=[kb_])
                for g8 in range(2):
                    bank = 4 + (2 * i + g8) % 4
                    pst = ps[bank][:, :].bitcast(BF16)
                    for q in range(8):
                        jj = g8 * 8 + q
                        p.add("pe", lambda e, b2=b2, jj=jj, q=q, pst=pst, npart=npart: e.transpose(pst[:, q * 128:q * 128 + npart], hb[b2][0:npart, jj * 128:(jj + 1) * 128], identb[0:npart, 0:npart]),
                              reads=[kb_, "identb"], writes=[("ps", bank)])
                    j0 = hf * 16 + g8 * 8
                    dst = hT_dst(blk, j0, 8)
                    eng = "act" if (g8 == 0) else "dve"
                    srcv = pst[:, :].rearrange("p (q c) -> p q c", c=128)[:, :, 0:npart]
                    if eng == "act":
                        p.add("act", lambda e, dst=dst, srcv=srcv: e.activation(dst, srcv, AF.Copy), reads=[("ps", bank)], writes=[("hT", blk)])
                    else:
                        p.add("dve", lambda e, dst=dst, srcv=srcv: e.tensor_copy(dst, srcv), reads=[("ps", bank)], writes=[("hT", blk)])

    def hT_tiles(lo, hi):
        out = []
        if lo < 128:
            out.append((lambda j: hTh[:, j, :], 0, 128))
            lo = 128
        for (c, n) in col_tiles(lo, hi):
            out.append((lambda j, c=c, n=n: hTm[:, j, c - 128:c - 128 + n], c, n))
        return out

    def proj_fm(l, nb_idx, lo, hi, banks):
        s = ring_load(w_in[l, nb_idx])
        res = []
        for ti, (fn, c, n) in enumerate(hT_tiles(lo, hi)):
            b = banks[ti]
            rk = [("ring", s)] + blk_keys("hT", c, n)
            for j in range(NJ):
                mm(ps[b][:, 0:n], wr[:, s, j, :], fn(j), j == 0, j == NJ - 1, reads=rk, writes=[("ps", b)])
            res.append((b, c, n))
        return res, s

    def phase_attn(l):
        zz = Zone(Z0)
        tag = U("at")
        kvlo = 128 * l
        flo = 128 * (l + 1)
        nkb = NPB - l
        kT = zz.sb(U("kT"), [128, 2, NT], BF16)
        vpad = zz.sb(U("vpad"), [128, 11, 2, 2, 128], BF16)
        qc = [zz.sb(U("qc"), [128, NT], BF16) for _ in range(2)]
        at = [zz.sb(U("atc"), [128, NT], BF16) for _ in range(2)]
        pt = [zz.sb(U("pt"), [128, 2, 256], BF16) for _ in range(3)]
        dtm = zz.sb(U("dtm"), [128, 512], F32)
        ot = [zz.sb(U("ot"), [128, 256], F32) for _ in range(2)]
        prep_sample(l, zz)
        p.add("dve", lambda e: e.memset(vpad[:].rearrange("p a b c d -> p (a b c d)"), 0.0), writes=[(tag, "vpad")])
        p.dma("sp", esink[:], sinks_in[l], writes=[(tag, "esink")], chan=U("es"))
        p.add("act", lambda e: e.activation(esink[:], esink[:], AF.Exp), reads=[(tag, "esink")], writes=[(tag, "esink")])
        for m in range(2):
            res, s = proj_fm(l, m, kvlo, NT, [0, 1, 2, 3])
            for (b, c, n) in res:
                p.add("act", lambda e, b=b, c=c, n=n, m=m: e.activation(kT[:, m, c:c + n], ps[b][:, 0:n], AF.Copy),
                      reads=[("ps", b)], writes=[(tag, "kT", m)])
            for (blk, npart, bank, dstap) in ((9, 128, 4, o_wk[l, :, m * 128:(m + 1) * 128]), (10, NS, 5, o_nk[l, :, m * 128:(m + 1) * 128])):
                for j in range(NJ):
                    lh = hT_dst(blk, j, 1)
                    mm(ps[bank][0:npart, 0:128], lh.rearrange("p a c -> p (a c)"), wr[:, s, j, :], j == 0, j == NJ - 1,
                       reads=[("ring", s), ("hT", blk)], writes=[("ps", bank)])
                o = ot[0 if blk == 9 else 1]
                ko = (tag, "ot", blk)
                p.add("dve", lambda e, o=o, bank=bank, npart=npart: e.tensor_copy(o[0:npart, 0:128], ps[bank][0:npart, 0:128]), reads=[("ps", bank)], writes=[ko])
                p.dma("sp", dstap, o[0:npart, 0:128], reads=[ko], chan=U("ok"))
        vblocks = list(range(l, NPB)) + [10]
        for m in range(2):
            s = ring_load(w_in[l, 2 + m])
            for i, blk in enumerate(vblocks):
                npart = NS if blk == 10 else 128
                bank = i // 4
                q = i % 4
                for j in range(NJ):
                    lh = hT_dst(blk, j, 1)
                    mm(ps[bank][0:npart, q * 128:(q + 1) * 128], lh.rearrange("p a c -> p (a c)"), wr[:, s, j, :], j == 0, j == NJ - 1,
                       reads=[("ring", s), ("hT", blk)], writes=[("ps", bank)])
            for i, blk in enumerate(vblocks):
                npart = NS if blk == 10 else 128
                bank = i // 4
                q = i % 4
                for hh in range(2):
                    p.add("act", lambda e, blk=blk, npart=npart, bank=bank, q=q, hh=hh, m=m: e.activation(
                        vpad[0:npart, blk, m, hh, hh * 64:hh * 64 + 64], ps[bank][0:npart, q * 128 + hh * 64:q * 128 + hh * 64 + 64], AF.Copy),
                        reads=[("ps", bank), (tag, "vpad")], writes=[(tag, "vpad", blk)])
                if blk in (9, 10):
                    o = ot[0 if blk == 9 else 1]
                    ko = (tag, "ot", blk)
                    dstap = (o_wv if blk == 9 else o_nv)[l, :, m * 128:(m + 1) * 128]
                    p.add("dve", lambda e, o=o, bank=bank, npart=npart, q=q: e.tensor_copy(o[0:npart, 128:256], ps[bank][0:npart, q * 128:(q + 1) * 128]),
                          reads=[("ps", bank)], writes=[ko])
                    p.dma("sp", dstap, o[0:npart, 128:256], reads=[ko], chan=U("ov"))
        for c in range(16):
            m = c // 8
            b2 = c % 2
            kq, ka = (tag, "qc", b2), (tag, "at", b2)
            res, _ = proj_fm(l, 4 + c, flo, NT, [0, 1, 2])
            for (b, c0, n) in res:
                p.add("dve", lambda e, b=b, c0=c0, n=n, b2=b2: e.tensor_copy(qc[b2][:, c0:c0 + n], ps[b][:, 0:n]), reads=[("ps", b)], writes=[kq])
            res, _ = proj_fm(l, 20 + c, flo, NT, [3, 4, 5])
            for (b, c0, n) in res:
                p.add("act", lambda e, b=b, c0=c0, n=n, b2=b2: e.activation(at[b2][:, c0:c0 + n], ps[b][:, 0:n], AF.Silu), reads=[("ps", b)], writes=[ka])
            def acol(cabs):
                r = cabs - flo
                return r // 512, r % 512
            prev = None
            pi = 0
            first_q = {}
            for kb in range(l, NPB):
                qbs = [qb for qb in (kb, kb + 1) if l + 1 <= qb <= NPB - 1]
                if not qbs:
                    continue
                qlo = qbs[0] * 128
                nq = 128 * len(qbs)
                mk = mask01 if kb <= 1 else maskA
                mk_key = "mask01" if kb <= 1 else "maskA"
                moff = 0 if qbs[0] == kb else 128
                cur = []
                for hh in range(2):
                    sb_ = 6 + (pi % 2)
                    half = ((pi // 2) % 2) * 256
                    pi += 1
                    pr = slice(hh * 64, hh * 64 + 64)
                    mm(ps[sb_][:, half:half + nq], kT[pr, m, kb * 128:(kb + 1) * 128], qc[b2][pr, qlo:qlo + nq], True, False,
                       reads=[(tag, "kT", m), kq], writes=[("ps", sb_)])
                    mm(ps[sb_][:, half:half + nq], identb[:, :], mk[:, moff:moff + nq], False, True,
                       reads=["identb", mk_key], writes=[("ps", sb_)])
                    ptile = pt[(kb * 2 + hh) % 3] if False else None
                    cur.append((sb_, half))
                pti = kb % 3
                for hh in range(2):
                    sb_, half = cur[hh]
                    p.add("act", lambda e, pti=pti, hh=hh, sb_=sb_, half=half, nq=nq: e.activation(pt[pti][:, hh, 0:nq], ps[sb_][:, half:half + nq], AF.Exp, scale=0.125),
                          reads=[("ps", sb_)], writes=[(tag, "pt", pti)])
                for qi, qb in enumerate(qbs):
                    bk, co = acol(qb * 128)
                    is_first = qb not in first_q
                    first_q[qb] = True
                    is_last = (qb == kb)
                    po = qi * 128
                    for hh in range(2):
                        st_ = is_first and hh == 0
                        sp_ = is_last and hh == 1
                        mm(ps[bk][:, co:co + 128], vpad[:, kb, m, hh, :], pt[pti][:, hh, po:po + 128], st_, sp_,
                           reads=[(tag, "vpad", kb), (tag, "pt", pti)], writes=[("ps", bk)])
                        mm(ps[3 + bk][:, co:co + 128], onesp[:, hh, :], pt[pti][:, hh, po:po + 128], st_, sp_,
                           reads=["onesp", (tag, "pt", pti)], writes=[("ps", 3 + bk)])
            bk, co = acol(NPB * 128)
            sample_attn(l, c, m, b2, tag, kT, vpad, qc, pt, bk, co, zz)
            ncols = NT - flo
            for t0 in range(0, ncols, 512):
                n = min(512, ncols - t0)
                bk = t0 // 512
                ca = flo + t0
                p.add("dve", lambda e, bk=bk, n=n, c=c: e.tensor_scalar(dtm[:, 0:n], ps[3 + bk][:, 0:n], esink[:, c:c + 1], None, op0=ALU.add),
                      reads=[("ps", 3 + bk), (tag, "esink")], writes=[(tag, "dtm")])
                p.add("dve", lambda e, n=n: e.reciprocal(dtm[:, 0:n], dtm[:, 0:n]), reads=[(tag, "dtm")], writes=[(tag, "dtm")])
                p.add("dve", lambda e, bk=bk, n=n: e.tensor_tensor(out=dtm[:, 0:n], in0=ps[bk][:, 0:n], in1=dtm[:, 0:n], op=ALU.mult),
                      reads=[("ps", bk), (tag, "dtm")], writes=[(tag, "dtm")])
                p.add("dve", lambda e, n=n, ca=ca, b2=b2: e.tensor_tensor(out=at[b2][:, ca:ca + n], in0=dtm[:, 0:n], in1=at[b2][:, ca:ca + n], op=ALU.mult),
                      reads=[(tag, "dtm"), ka], writes=[ka])
            p.dma("sp", spA[c, :, flo - 128:NM], at[b2][:, flo:NT], reads=[ka], writes=[("spA", c)], chan=f"{tag}sa{b2}")
            if DEBUG and l == 0:
                p.dma("sp", dbg["A"][c, :, :], at[b2][:, 128:NT], reads=[ka], chan=U("dbgA"))

    def sample_attn(l, c, m, b2, tag, kT, vpad, qc, pt, bk, co, zz):
        st = S_at[l]
        kq = (tag, "qc", b2)
        scol = NPB * 128
        for hh in range(2):
            pr = slice(hh * 64, hh * 64 + 64)
            mm(ps[7][0:NS, 256 + hh * 16:256 + hh * 16 + NS], kT[pr, m, scol:scol + NS], qc[b2][pr, scol:scol + NS], True, False,
               reads=[(tag, "kT", m), kq], writes=[("ps", 7)])
            mm(ps[7][0:NS, 256 + hh * 16:256 + hh * 16 + NS], identb[0:NS, 0:NS], st["smask"][0:NS, 1, 0:NS], False, True,
               reads=["identb", "smask"], writes=[("ps", 7)])
        p.add("act", lambda e: e.activation(st["po"][0:NS, 0:32], ps[7][0:NS, 256:288], AF.Exp, scale=0.125),
              reads=[("ps", 7)], writes=[(tag, "po")])
        for sq in range(4):
            for hh in range(2):
                pr = slice(hh * 64, hh * 64 + 64)
                cc = 320 + hh * 16 + sq * 4
                mm(ps[7][:, cc:cc + 4], st["ckT"][pr, sq, m, :], qc[b2][pr, scol + sq * 4:scol + sq * 4 + 4], True, False,
                   reads=[("ckT", l), kq], writes=[("ps", 7)])
                mm(ps[7][:, cc:cc + 4], identb[:, :], st["smask"][:, 0, 0:4], False, True,
                   reads=["identb", "smask"], writes=[("ps", 7)])
        p.add("act", lambda e: e.activation(st["pc"][:, 0:32], ps[7][:, 320:352], AF.Exp, scale=0.125),
              reads=[("ps", 7)], writes=[(tag, "pc")])
        for hh in range(2):
            mm(ps[bk][:, co:co + NS], vpad[0:NS, 10, m, hh, :], st["po"][0:NS, hh * 16:hh * 16 + NS], hh == 0, False,
               reads=[(tag, "vpad", 10), (tag, "po")], writes=[("ps", bk)])
            mm(ps[3 + bk][:, co:co + NS], onesp[0:NS, hh, :], st["po"][0:NS, hh * 16:hh * 16 + NS], hh == 0, False,
               reads=["onesp", (tag, "po")], writes=[("ps", 3 + bk)])
        for sq in range(4):
            for hh in range(2):
                lastmm = (sq == 3 and hh == 1)
                cc = hh * 16 + sq * 4
                mm(ps[bk][:, co + sq * 4:co + sq * 4 + 4], st["cvp"][:, sq, m, hh, :], st["pc"][:, cc:cc + 4], False, lastmm,
                   reads=[("cvp", l), (tag, "pc")], writes=[("ps", bk)])
                mm(ps[3 + bk][:, co + sq * 4:co + sq * 4 + 4], onesp[:, hh, :], st["pc"][:, cc:cc + 4], False, lastmm,
                   reads=["onesp", (tag, "pc")], writes=[("ps", 3 + bk)])

    S_at = {}

    def prep_sample(l, zz):
        tag = U("ps")
        s_ckT = zz.sb(U("s_ckT"), [128, 4, 2, 128], BF16)
        s_cvp = zz.sb(U("s_cvp"), [128, 4, 2, 2, 128], BF16)
        cf = zz.sb(U("cf"), [128, 4, 256], F32)
        vf = zz.sb(U("vf"), [128, 4, 256], F32)
        S_at[l] = dict(smask=smask, po=s_po, pc=s_pc, ckT=s_ckT, cvp=s_cvp)
        p.dma("sp", cf[:], ck_in[l].rearrange("s j f -> j s f"), writes=[(tag, "cf")], chan=U("ck"))
        p.dma("sp", vf[:], cv_in[l].rearrange("s j f -> j s f"), writes=[(tag, "vf")], chan=U("cv"))
        p.add("dve", lambda e: e.memset(s_cvp[:].rearrange("p a b c d -> p (a b c d)"), 0.0), writes=[("cvp", l)])
        for sq in range(4):
            for m in range(2):
                for hh in range(2):
                    p.add("dve", lambda e, sq=sq, m=m, hh=hh: e.tensor_copy(s_cvp[:, sq, m, hh, hh * 64:hh * 64 + 64], vf[:, sq, m * 128 + hh * 64:m * 128 + hh * 64 + 64]),
                          reads=[(tag, "vf")], writes=[("cvp", l)])
                p.add("pe", lambda e, sq=sq, m=m: e.transpose(ps[6][:, 0:128], cf[:, sq, m * 128:(m + 1) * 128], identf[:, :]),
                      reads=[(tag, "cf"), "identf"], writes=[("ps", 6)])
                p.add("dve", lambda e, sq=sq, m=m: e.tensor_copy(s_ckT[:, sq, m, :], ps[6][:, 0:128]), reads=[("ps", 6)], writes=[("ckT", l)])


    def phase_sgu(l):
        tag = U("sg")
        flo = 128 * (l + 1)
        fblocks = list(range(l + 1, NPB)) + [10]
        nfb = len(fblocks)
        vbnS = sb_at(U("vbn"), [128, 10, 2048], BF16, ZH)
        raw = sb_at(U("raw"), [128, 4, 2048], F32, ZH + 40960)
        z2 = Zone(ZH + 40960 + 32768)
        tmp = z2.sb(U("tmp"), [128, 1024], F32)
        sgt = z2.sb(U("sgt"), [128, 1024], F32)
        sbt = z2.sb(U("sbt"), [128, 1024], F32)
        bst = z2.sb(U("bst"), [128, 4, 6], F32)
        mv = z2.sb(U("mv"), [128, 10, 2], F32)
        zr = Zone(ZH + 40960)
        wsf = zr.sb(U("wsf"), [128, 128], F32)
        bsr_f = zr.sb(U("bsr_f"), [1, 8, 128], F32)
        bsr_t = zr.sb(U("bsr_t"), [1, 8, 128], F32)
        bss_f = zr.sb(U("bss_f"), [1, 8, 16], F32)
        bss_t = zr.sb(U("bss_t"), [1, 8, 16], F32)
        assert z2.o <= PZ, z2.o
        for g in range(8):
            p.dma("sp", wsf[:], w_s[l, g], writes=[(tag, "wsf")], chan=U("ws"))
            p.add("dve", lambda e: e.tensor_tensor(out=wsf[:], in0=wsf[:], in1=trilf[:], op=ALU.mult), reads=[(tag, "wsf"), "trilf"], writes=[(tag, "wsf")])
            p.add("pe", lambda e: e.transpose(ps[6][:, 0:128], wsf[:, :], identf[:, :]), reads=[(tag, "wsf"), "identf"], writes=[("ps", 6)])
            p.add("dve", lambda e, g=g: e.tensor_copy(WmT[:, g, :], ps[6][:, 0:128]), reads=[("ps", 6)], writes=[(tag, "WmT")])
        p.add("dve", lambda e: e.memset(WsT[:].rearrange("p g t -> p (g t)"), 0.0), writes=[(tag, "WsT")])
        gws = p.dma_group(U("wst"))
        for sq in range(4):
            p.dma("sp", WsT[sq * 4:sq * 4 + 4, :, sq * 4:sq * 4 + 4], WmT[0:4, :, 0:4], reads=[(tag, "WmT"), (tag, "WsT")], writes=[(tag, "WsT", sq)], group=gws,
                  allow_slow_non_contiguous=True)
        gb = p.dma_group(U("bs"))
        p.dma("sp", bsr_f[0:1, :, :], b_s[l:l + 1, :, :], writes=[(tag, "bsrf")], group=gb)
        for sq in range(4):
            p.dma("sp", bss_f[0:1, :, sq * 4:sq * 4 + 4], b_s[l:l + 1, :, 0:4], writes=[(tag, "bssf", sq)], group=gb, allow_slow_non_contiguous=True)
        for (f_, hi_, lo_, t_, kk) in ((bsr_f, bsr_hi, bsr_lo, bsr_t, "bsrf"), (bss_f, bss_hi, bss_lo, bss_t, "bssf")):
            rk_ = [(tag, kk)] + [(tag, kk, x) for x in range(4)]
            p.add("dve", lambda e, f_=f_, hi_=hi_: e.tensor_copy(hi_[:], f_[:]), reads=rk_, writes=[(tag, kk, "hi")])
            p.add("dve", lambda e, hi_=hi_, t_=t_: e.tensor_copy(t_[:], hi_[:]), reads=[(tag, kk, "hi")], writes=[(tag, kk, "t")])
            p.add("dve", lambda e, f_=f_, t_=t_: e.tensor_tensor(out=t_[:], in0=f_[:], in1=t_[:], op=ALU.subtract), reads=rk_ + [(tag, kk, "t")], writes=[(tag, kk, "t")])
            p.add("dve", lambda e, lo_=lo_, t_=t_: e.tensor_copy(lo_[:], t_[:]), reads=[(tag, kk, "t")], writes=[(tag, kk, "lo")])
        barrier()
        for pas in range(3):
            pblocks = list(enumerate(fblocks))[pas * 4:(pas + 1) * 4]
            if not pblocks:
                continue
            for i8 in range(8):
                s = ring_pair_load(w_in[l, 36 + 2 * i8:36 + 2 * i8 + 2])
                pb0 = (i8 % 2) * 2
                for li, (i, blk) in enumerate(pblocks):
                    npart = NS if blk == 10 else 128
                    bank = pb0 + li // 2
                    q = li % 2
                    for j in range(NJ):
                        lh = hT_dst(blk, j, 1)
                        mm(ps[bank][0:npart, q * 256:(q + 1) * 256].rearrange("p (a c) -> p a c", c=128), lh.rearrange("p a c -> p (a c)"), wr[:, s:s + 2, j, :], j == 0, j == NJ - 1,
                           reads=[("ring", s), ("ring", s + 1), ("hT", blk)], writes=[("ps", bank)])
                for li, (i, blk) in enumerate(pblocks):
                    npart = NS if blk == 10 else 128
                    bank = pb0 + li // 2
                    q = li % 2
                    if li % 2 == 0:
                        p.add("act", lambda e, li=li, q=q, bank=bank, i8=i8, npart=npart: e.activation(raw[0:npart, li, i8 * 256:(i8 + 1) * 256], ps[bank][0:npart, q * 256:(q + 1) * 256], AF.Copy),
                              reads=[("ps", bank)], writes=[(tag, "raw", li)])
                    else:
                        p.add("dve", lambda e, li=li, q=q, bank=bank, i8=i8, npart=npart: e.tensor_copy(raw[0:npart, li, i8 * 256:(i8 + 1) * 256], ps[bank][0:npart, q * 256:(q + 1) * 256]),
                              reads=[("ps", bank)], writes=[(tag, "raw", li)])
            for li, (i, blk) in enumerate(pblocks):
                npart = NS if blk == 10 else 128
                for q in range(4):
                    p.add("dve", lambda e, li=li, q=q, npart=npart: e.bn_stats(bst[0:npart, q, :], raw[0:npart, li, q * 512:(q + 1) * 512]),
                          reads=[(tag, "raw", li)], writes=[(tag, "bst")])
                p.add("dve", lambda e, npart=npart: e.bn_aggr(smallf[0:npart, 0:2], bst[0:npart, :, :]), reads=[(tag, "bst")], writes=[(tag, "mvt")])
                p.add("dve", lambda e, npart=npart: e.tensor_scalar_add(smallf[0:npart, 1:2], smallf[0:npart, 1:2], LN_EPS), reads=[(tag, "mvt")], writes=[(tag, "mvt")])
                p.add("act", lambda e, npart=npart: e.activation(smallf[0:npart, 2:3], smallf[0:npart, 1:2], AF.Sqrt), reads=[(tag, "mvt")], writes=[(tag, "mvs")])
                p.add("dve", lambda e, i=i, npart=npart: e.reciprocal(mv[0:npart, i, 1:2], smallf[0:npart, 2:3]),
                      reads=[(tag, "mvs")], writes=[(tag, "mv", i)])
                p.add("dve", lambda e, i=i, npart=npart: e.tensor_copy(mv[0:npart, i, 0:1], smallf[0:npart, 0:1]), reads=[(tag, "mvt")], writes=[(tag, "mv", i)])
            for hf in range(2):
                c0 = hf * 1024
                g = p.dma_group(U("sgp"))
                p.dma("sp", sgt[:], sgu_g[l:l + 1, c0:c0 + 1024].broadcast_to([128, 1024]), writes=[(tag, "sgt")], group=g)
                p.dma("sp", sbt[:], sgu_b[l:l + 1, c0:c0 + 1024].broadcast_to([128, 1024]), writes=[(tag, "sbt")], group=g)
                for li, (i, blk) in enumerate(pblocks):
                    npart = NS if blk == 10 else 128
                    p.add("dve", lambda e, li=li, i=i, npart=npart, c0=c0: e.tensor_scalar(tmp[0:npart, :], raw[0:npart, li, c0:c0 + 1024], mv[0:npart, i, 0:1], mv[0:npart, i, 1:2], op0=ALU.subtract, op1=ALU.mult),
                          reads=[(tag, "raw", li), (tag, "mv", i)], writes=[(tag, "tmp")])
                    p.add("dve", lambda e, npart=npart: e.tensor_tensor(out=tmp[0:npart, :], in0=tmp[0:npart, :], in1=sgt[0:npart, :], op=ALU.mult),
                          reads=[(tag, "tmp"), (tag, "sgt")], writes=[(tag, "tmp")])
                    if blk == 10:
                        p.add("dve", lambda e, npart=npart: e.tensor_tensor(out=tmp[0:npart, :], in0=tmp[0:npart, :], in1=sbt[0:npart, :], op=ALU.add),
                              reads=[(tag, "tmp"), (tag, "sbt")], writes=[(tag, "tmp")])
                        p.dma("sp", o_sv[l, :, c0:c0 + 1024], tmp[0:NS, :], reads=[(tag, "tmp")], writes=[(tag, "osv")], chan=U("osv"))
                        p.add("dve", lambda e, i=i, npart=npart, c0=c0: e.tensor_copy(vbnS[0:npart, i, c0:c0 + 1024], tmp[0:npart, :]),
                              reads=[(tag, "tmp")], writes=[(tag, "vbn", i)])
                    else:
                        p.add("dve", lambda e, i=i, npart=npart, c0=c0: e.tensor_tensor(out=vbnS[0:npart, i, c0:c0 + 1024], in0=tmp[0:npart, :], in1=sbt[0:npart, :], op=ALU.add),
                              reads=[(tag, "tmp"), (tag, "sbt")], writes=[(tag, "vbn", i)])
        barrier()
        BT = sb_at("BT_%d" % l, [128, 16, NM], BF16, ZH + 40960)
        z3 = Zone(ZH + 40960 + 37376)
        ut = z3.sb(U("ut"), [128, NM], F32)
        szb = z3.sb(U("szb"), [128, NM], F32)
        ncols = NT - flo
        for j16 in range(16):
            g8 = j16 // 2
            if l == 0:
                mod_group(0, 16 + j16 // 2, 6 + (j16 // 2) % 2, halves=(j16 % 2,))
            res, _ = proj_fm(l, 68 + j16, flo, NT, [3, 4, 5])
            for (b, c0, n) in res:
                p.add("act", lambda e, b=b, c0=c0, n=n: e.activation(szb[:, c0 - flo:c0 - flo + n], ps[b][:, 0:n], AF.Silu), reads=[("ps", b)], writes=[(tag, "szb")])
            res, _ = proj_fm(l, 52 + j16, flo, NT, [0, 1, 2])
            for (b, c0, n) in res:
                p.add("act", lambda e, b=b, c0=c0, n=n: e.activation(ut[:, c0 - flo:c0 - flo + n], ps[b][:, 0:n], AF.Copy), reads=[("ps", b)], writes=[(tag, "ut")])
            tiles = [(t0, min(512, ncols - t0)) for t0 in range(0, ncols, 512)]
            for (t0, n) in tiles:
                bank = 3 + t0 // 512
                first = True
                cpos = t0
                while cpos < t0 + n:
                    cabs = flo + cpos
                    is_s = cabs >= NPB * 128
                    w = NS if is_s else 128
                    bi = fblocks.index(10 if is_s else cabs // 128)
                    lastc = (cpos + w >= t0 + n)
                    if is_s:
                        mm(ps[bank][:, cpos - t0:cpos - t0 + w], vbnS[0:NS, bi, j16 * 128:(j16 + 1) * 128], WsT[0:NS, g8, :], True, False,
                           reads=[(tag, "vbn", bi)] + [(tag, "WsT", x) for x in range(4)], writes=[("ps", bank)])
                        mm(ps[bank][:, cpos - t0:cpos - t0 + w], ones1[0:1, :], bss_hi[0:1, g8, :], False, False, reads=["ones1", (tag, "bssf", "hi")], writes=[("ps", bank)])
                        mm(ps[bank][:, cpos - t0:cpos - t0 + w], ones1[0:1, :], bss_lo[0:1, g8, :], False, True, reads=["ones1", (tag, "bssf", "lo")], writes=[("ps", bank)])
                    else:
                        mm(ps[bank][:, cpos - t0:cpos - t0 + w], vbnS[:, bi, j16 * 128:(j16 + 1) * 128], WmT[:, g8, :], True, False,
                           reads=[(tag, "vbn", bi), (tag, "WmT")], writes=[("ps", bank)])
                        mm(ps[bank][:, cpos - t0:cpos - t0 + w], ones1[0:1, :], bsr_hi[0:1, g8, :], False, False, reads=["ones1", (tag, "bsrf", "hi")], writes=[("ps", bank)])
                        mm(ps[bank][:, cpos - t0:cpos - t0 + w], ones1[0:1, :], bsr_lo[0:1, g8, :], False, True, reads=["ones1", (tag, "bsrf", "lo")], writes=[("ps", bank)])
                    cpos += w
                p.add("dve", lambda e, bank=bank, t0=t0, n=n: e.tensor_tensor(out=ut[:, t0:t0 + n], in0=ps[bank][:, 0:n], in1=ut[:, t0:t0 + n], op=ALU.mult),
                      reads=[("ps", bank), (tag, "ut")], writes=[(tag, "ut")])
                p.add("dve", lambda e, t0=t0, n=n, j16=j16: e.tensor_tensor(out=BT[:, j16, flo - 128 + t0:flo - 128 + t0 + n], in0=ut[:, t0:t0 + n], in1=szb[:, t0:t0 + n], op=ALU.mult),
                      reads=[(tag, "ut"), (tag, "szb")], writes=[("BT", j16)])
        if DEBUG and l == 0:
            p.dma("sp", dbg["B"], BT[:].rearrange("p a c -> p (a c)"), reads=[("BT", j) for j in range(16)], chan=U("dbgB"))
        return BT

    def phase_merge(l, BT):
        tag = U("mg")
        flo = 128 * (l + 1)
        ncols = NT - flo
        AT = sb_at("AT_%d" % l, [128, 16, NM], BF16, ZH)
        z3 = Zone(ZH + 40960 + 37376)
        sg = z3.sb(U("sg"), [128, NM], F32)
        tA = z3.sb(U("tA"), [128, NM], F32)
        mt = [z3.sb(U("mt"), [128, NM], BF16) for _ in range(2)]
        ga = p.dma_group(U("ald"))
        for c in range(16):
            p.dma("sp", AT[:, c, flo - 128:NM], spA[c, :, flo - 128:NM], reads=[("spA", c)], writes=[("AT", c)], group=ga)

        def act_tiles(buf, lo):
            out = []
            for (c0, n) in col_tiles(flo, NT):
                out.append((c0, n))
            return out

        for j in range(32):
            b2 = j % 2
            km = (tag, "mt", b2)
            if l == 0 and j < 24:
                mod_group(1, j, 6 + j % 2, halves=(0,))
            res, _ = proj_fm(l, 84 + j, flo, NT, [0, 1, 2])
            for (b, c0, n) in res:
                p.add("act", lambda e, b=b, c0=c0, n=n: e.activation(sg[:, c0 - flo:c0 - flo + n], ps[b][:, 0:n], AF.Sigmoid), reads=[("ps", b)], writes=[(tag, "sg")])
            s = ring_load(w_pab[l, j])
            for ti, (c0, n) in enumerate(col_tiles(flo, NT)):
                b = 3 + ti
                for k in range(16):
                    mm(ps[b][:, 0:n], wr[:, s, k, :], AT[:, k, c0 - 128:c0 - 128 + n], k == 0, k == 15,
                       reads=[("ring", s)] + [("AT", k)], writes=[("ps", b)])
                p.add("dve", lambda e, b=b, c0=c0, n=n: e.tensor_tensor(out=tA[:, c0 - flo:c0 - flo + n], in0=ps[b][:, 0:n], in1=sg[:, c0 - flo:c0 - flo + n], op=ALU.mult),
                      reads=[("ps", b), (tag, "sg")], writes=[(tag, "tA")])
            if l == 0 and j < 24:
                mod_group(1, j, 6 + j % 2, halves=(1,))
            res, _ = proj_fm(l, 116 + j, flo, NT, [0, 1, 2])
            for (b, c0, n) in res:
                p.add("act", lambda e, b=b, c0=c0, n=n: e.activation(sg[:, c0 - flo:c0 - flo + n], ps[b][:, 0:n], AF.Sigmoid), reads=[("ps", b)], writes=[(tag, "sg")])
            for ti, (c0, n) in enumerate(col_tiles(flo, NT)):
                b = 3 + ti
                for k in range(16):
                    mm(ps[b][:, 0:n], wr[:, s, 16 + k, :], BT[:, k, c0 - 128:c0 - 128 + n], k == 0, k == 15,
                       reads=[("ring", s)] + [("BT", k)], writes=[("ps", b)])
                p.add("dve", lambda e, b=b, c0=c0, n=n: e.tensor_tensor(out=sg[:, c0 - flo:c0 - flo + n], in0=ps[b][:, 0:n], in1=sg[:, c0 - flo:c0 - flo + n], op=ALU.mult),
                      reads=[("ps", b), (tag, "sg")], writes=[(tag, "sg")])
                p.add("dve", lambda e, c0=c0, n=n, b2=b2: e.tensor_tensor(out=mt[b2][:, c0 - flo:c0 - flo + n], in0=sg[:, c0 - flo:c0 - flo + n], in1=tA[:, c0 - flo:c0 - flo + n], op=ALU.add),
                      reads=[(tag, "sg"), (tag, "tA")], writes=[km])
            p.dma("sp", spM[j, :, flo - 128:NM], mt[b2][:, 0:ncols], reads=[km], writes=[("spM", j)], chan=f"{tag}m{b2}")
            if DEBUG and l == 0:
                p.dma("sp", dbg["M"][j, :, :], mt[b2][:, 0:ncols], reads=[km], chan=U("dbgM"))

    def phase_out(l, xsrc):
        tag = U("po")
        flo = 128 * (l + 1)
        fblocks = list(range(l + 1, NPB)) + [10]
        nfb = len(fblocks)
        npb_f = nfb - 1
        mT = hTm
        gm = p.dma_group(U("mld"))
        for j in range(32):
            p.dma("sp", mT[:, j, flo - 128:NM], spM[j, :, flo - 128:NM], reads=[("spM", j)], writes=[("mT", j)], group=gm)
        zz = Zone(ZH)
        gt = zz.sb(U("gt"), [128, D], F32)
        gts = zz.sb(U("gts"), [NS, D], F32)
        CW = 256
        xt = [zz.sb(U("xt"), [128, 10, CW], F32) for _ in range(2)]
        xs = [zz.sb(U("xs"), [NS, CW], F32) for _ in range(2)]
        rt = [zz.sb(U("rt"), [128, 10, CW], F32) for _ in range(2)]
        rs = [zz.sb(U("rs"), [NS, CW], F32) for _ in range(2)]
        gg = p.dma_group(U("gld"))
        p.dma("sp", gt[:], modd[l, 16:17, 2 * D:3 * D].broadcast_to([128, D]), reads=[("modd", l)], writes=[(tag, "gt")], group=gg)
        p.dma("sp", gts[:], modd[l, 0:NS, 2 * D:3 * D], reads=[("modd", l)], writes=[(tag, "gts")], group=gg)
        r0 = flo
        def issue_x(g_):
            b2_ = g_ % 2
            c0_ = g_ * CW
            gx = p.dma_group(f"{tag}x{b2_}")
            p.dma("sp", xt[b2_][:, 0:npb_f, :], xsrc[r0:r0 + npb_f * 128, c0_:c0_ + CW].rearrange("(t p) c -> p t c", p=128),
                  reads=[("xsrc",)], writes=[(tag, "xt", b2_)], group=gx)
            p.dma("sp", xs[b2_][:], xsrc[NPB * 128:NT, c0_:c0_ + CW], reads=[("xsrc",)], writes=[(tag, "xs", b2_)], group=gx)
        issue_x(0)
        for g in range(D // CW):
            b2 = g % 2
            c0 = g * CW
            kx, kr = (tag, "xt", b2), (tag, "rt", b2)
            kxs, krs = (tag, "xs", b2), (tag, "rs", b2)
            if g + 1 < D // CW:
                issue_x(g + 1)
            s = ring_pair_load(w_o[l, 2 * g:2 * g + 2])
            for i, blk in enumerate(fblocks):
                npart = NS if blk == 10 else 128
                bank = i // 2
                q = i % 2
                if blk == 10:
                    lhs = lambda j: mT[:, j, 1152:1168]
                else:
                    lhs = lambda j, blk=blk: mT[:, j, (blk - 1) * 128:blk * 128]
                for j in range(NJ):
                    mm(ps[bank][0:npart, q * CW:(q + 1) * CW].rearrange("p (a c) -> p a c", c=128), lhs(j), wr[:, s:s + 2, j, :], j == 0, j == NJ - 1,
                       reads=[("ring", s), ("ring", s + 1), ("mT", j)], writes=[("ps", bank)])
            for i, blk in enumerate(fblocks):
                npart = NS if blk == 10 else 128
                bank = i // 2
                q = i % 2
                if blk == 10:
                    p.add("dve", lambda e, bank=bank, q=q, b2=b2, c0=c0: e.tensor_tensor(out=rs[b2][:, :], in0=ps[bank][0:NS, q * CW:(q + 1) * CW], in1=gts[:, c0:c0 + CW], op=ALU.mult),
                          reads=[("ps", bank), (tag, "gts")], writes=[krs])
                    p.add("dve", lambda e, b2=b2: e.scalar_tensor_tensor(out=rs[b2][:, :], in0=xs[b2][:, :], scalar=ALPHA, in1=rs[b2][:, :], op0=ALU.mult, op1=ALU.add),
                          reads=[kxs, krs], writes=[krs])
                else:
                    p.add("dve", lambda e, bank=bank, q=q, b2=b2, c0=c0, i=i: e.tensor_tensor(out=rt[b2][:, i, :], in0=ps[bank][:, q * CW:(q + 1) * CW], in1=gt[:, c0:c0 + CW], op=ALU.mult),
                          reads=[("ps", bank), (tag, "gt")], writes=[kr])
                    p.add("dve", lambda e, b2=b2, i=i: e.scalar_tensor_tensor(out=rt[b2][:, i, :], in0=xt[b2][:, i, :], scalar=ALPHA, in1=rt[b2][:, i, :], op0=ALU.mult, op1=ALU.add),
                          reads=[kx, kr], writes=[kr])
            go = p.dma_group(f"{tag}o{b2}")
            p.dma("sp", spR[r0:r0 + npb_f * 128, c0:c0 + CW].rearrange("(t p) c -> p t c", p=128), rt[b2][:, 0:npb_f, :],
                  reads=[kr], writes=[("spR",)], group=go)
            p.dma("sp", spR[NPB * 128:NT, c0:c0 + CW], rs[b2][:], reads=[krs], writes=[("spRs",)], group=go)
        if DEBUG and l == 0:
            pass

    p.dma("sp", tmpf[:, 512:768], smask_in.rearrange("p a c -> p (a c)"), writes=["c_t2"], chan="c2")
    p.add("dve", lambda e: e.tensor_copy(smask[:].rearrange("p a c -> p (a c)"), tmpf[:, 512:768]), reads=["c_t2"], writes=["smask"])
    barrier()
    mod_setup()
    barrier()
    for grp in range(16):
        mod_group(0, grp, grp % 2)
    barrier()
    phase_h(0, x_in, list(range(0, NPB)) + [10], False, None)
    if DEBUG:
        p.dma("sp", dbg["hT"], hTm[:].rearrange("p a c -> p (a c)"), reads=[("hT", b) for b in range(11)], chan=U("dbgh"))
    barrier()
    for l in range(DEPTH):
        phase_attn(l)
        barrier()
        BT = phase_sgu(l)
        barrier()
        phase_merge(l, BT)
        barrier()
        phase_out(l, x_in if l == 0 else spX)
        barrier()
        if l == 0:
            def xdst(blk):
                if blk == 10:
                    return spX[NPB * 128:NT, :]
                return spX[blk * 128:(blk + 1) * 128, :]
            phase_h(1, spR, list(range(1, NPB)) + [10], True, xdst)
        else:
            def ydst(blk):
                if blk == 10:
                    return y_s
                return y_p[(blk - 2) * 128:(blk - 1) * 128, :]
            phase_h(2, spR, list(range(2, NPB)) + [10], True, ydst)
        barrier()
    p.emit(nc)
    return nc


def _blocked(W, cw=128):
    K, N = W.shape
    return np.ascontiguousarray(W.reshape(K // 128, 128, N // cw, cw).transpose(2, 1, 0, 3).reshape(N // cw, 128, (K // 128) * cw))


_NC_CACHE = {}


def kernel(x_prompt, x_sample, cache_k, cache_v, c_prompt, c_sample, w_ada, b_ada, w_in, attn_sinks,
           sgu_ln_gain, sgu_ln_bias, sgu_w_s, sgu_b_s, w_pa, w_pb, w_o, ln_gain, ln_bias):
    f32 = np.float32
    A = lambda a: np.asarray(a, dtype=f32)
    x_prompt, x_sample, cache_k, cache_v = A(x_prompt), A(x_sample), A(cache_k), A(cache_v)
    c_prompt, c_sample, w_ada, b_ada, w_in = A(c_prompt), A(c_sample), A(w_ada), A(b_ada), A(w_in)
    attn_sinks, sgu_ln_gain, sgu_ln_bias, sgu_w_s, sgu_b_s = A(attn_sinks), A(sgu_ln_gain), A(sgu_ln_bias), A(sgu_w_s), A(sgu_b_s)
    w_pa, w_pb, w_o, ln_gain, ln_bias = A(w_pa), A(w_pb), A(w_o), A(ln_gain), A(ln_bias)

    heads = []
    for c in range(16):
        m, i = c // 8, c % 8
        heads += [(2 * m) * 8 + i, (2 * m + 1) * 8 + i]
    hperm = np.concatenate([np.arange(h * 64, h * 64 + 64) for h in heads])
    q0, k0, v0, za0, u0, vb0, zb0, ga0, gb0 = 0, 2048, 2304, 2560, 4608, 6656, 8704, 10752, 14848
    cols = np.concatenate([np.arange(k0, k0 + 256), np.arange(v0, v0 + 256), q0 + hperm, za0 + hperm,
                           np.arange(vb0, vb0 + 2048), np.arange(u0, u0 + 2048), np.arange(zb0, zb0 + 2048),
                           np.arange(ga0, ga0 + 4096), np.arange(gb0, gb0 + 4096)])
    w_in_b = np.stack([_blocked(w_in[l][:, cols]) for l in range(DEPTH)])
    w_ada_b = np.stack([_blocked(w_ada[l]) for l in range(DEPTH)])
    w_pab_b = np.stack([_blocked(np.concatenate([w_pa[l][hperm, :], w_pb[l]], axis=0)) for l in range(DEPTH)])
    w_o_b = np.stack([_blocked(w_o[l]) for l in range(DEPTH)])
    sinks_pp = np.zeros((DEPTH, 128, 16), f32)
    for l in range(DEPTH):
        for c in range(16):
            sinks_pp[l, 0:64, c] = attn_sinks[l, heads[2 * c]]
            sinks_pp[l, 64:128, c] = attn_sinks[l, heads[2 * c + 1]]
    identf = np.eye(128, dtype=f32)
    li = np.arange(128)[:, None]
    qi = np.arange(128)[None, :]
    msame = np.where(li <= qi, 0.0, NEG).astype(f32)
    mprev = np.where(li > qi, 0.0, NEG).astype(f32)
    maskA = np.concatenate([msame, mprev], axis=1)
    maskB = np.concatenate([msame, np.full((128, 128), NEG, f32)], axis=1)
    tril = np.tril(np.ones((128, 128), f32))
    smask = np.full((128, 2, 128), NEG, f32)
    smask[:, 0, 0:4] = np.where(li > np.arange(4)[None, :], 0.0, NEG)
    for a in range(16):
        for b in range(16):
            if a // 4 == b // 4 and a % 4 <= b % 4:
                smask[a, 1, b] = 0.0

    in_maps = []
    for cid in range(NCORES):
        b, half = cid // 2, cid % 2
        start = half * 1024
        x_core = np.zeros((NT, D), f32)
        lo = start - 256
        src_lo = max(lo, 0)
        x_core[src_lo - lo:NPB * 128] = x_prompt[b, src_lo:start + 1024]
        x_core[NPB * 128:] = x_sample[4 * cid:4 * cid + 4].reshape(NS, D)
        crow = np.zeros((32, D), f32)
        for r in range(NS):
            crow[r] = c_sample[4 * cid + r // 4]
        crow[16] = c_prompt[b]
        cT = np.ascontiguousarray(crow.reshape(32, NJ, 128).transpose(2, 1, 0).reshape(128, NJ * 32))
        ck = np.ascontiguousarray(cache_k[:, 4 * cid:4 * cid + 4].reshape(DEPTH, 4, 128, 256))
        cv = np.ascontiguousarray(cache_v[:, 4 * cid:4 * cid + 4].reshape(DEPTH, 4, 128, 256))
        in_maps.append({
            "x_in": x_core, "cT": cT, "ck": ck, "cv": cv, "w_ada": w_ada_b, "b_ada": b_ada, "w_in": w_in_b,
            "w_pab": w_pab_b, "w_o": w_o_b, "sinks": sinks_pp, "sgu_g": sgu_ln_gain, "sgu_b": sgu_ln_bias,
            "w_s": sgu_w_s, "b_s": sgu_b_s, "ln_g": ln_gain, "ln_b": ln_bias, "identf": identf,
            "maskA": maskA, "mask01": (maskA if half == 1 else maskB), "tril": tril, "smask": smask,
        })
    if "nc" not in _NC_CACHE:
        _NC_CACHE["nc"] = build_program()
    nc = _NC_CACHE["nc"]
    res = run_bass_kernel_spmd(nc, in_maps, core_ids=list(range(NCORES)))
    R = res.results
    kernel.last_results = R
    y_prompt = np.zeros((4, 2048, D), f32)
    y_sample = np.zeros((32, 4, D), f32)
    win_k = np.zeros((DEPTH, 4, 128, 4, 64), f32)
    win_v = np.zeros((DEPTH, 4, 128, 4, 64), f32)
    new_k = np.zeros((DEPTH, 32, 4, 4, 64), f32)
    new_v = np.zeros((DEPTH, 32, 4, 4, 64), f32)
    sgu_v = np.zeros((DEPTH, 32, 4, 2048), f32)
    for cid in range(NCORES):
        b, half = cid // 2, cid % 2
        r = R[cid]
        y_prompt[b, half * 1024:(half + 1) * 1024] = np.asarray(r["y_p"])
        y_sample[4 * cid:4 * cid + 4] = np.asarray(r["y_s"]).reshape(4, 4, D)
        if half == 1:
            win_k[:, b] = np.asarray(r["o_wk"]).reshape(DEPTH, 128, 4, 64)
            win_v[:, b] = np.asarray(r["o_wv"]).reshape(DEPTH, 128, 4, 64)
        new_k[:, 4 * cid:4 * cid + 4] = np.asarray(r["o_nk"]).reshape(DEPTH, 4, 4, 4, 64)
        new_v[:, 4 * cid:4 * cid + 4] = np.asarray(r["o_nv"]).reshape(DEPTH, 4, 4, 4, 64)
        sgu_v[:, 4 * cid:4 * cid + 4] = np.asarray(r["o_sv"]).reshape(DEPTH, 4, 4, 2048)
    return (y_prompt, y_sample, win_k, win_v, new_k, new_v, sgu_v)
```
